# Optimizing a Trainium2 kernel written in Bass

```python
import math
import jax, jax.numpy as jnp
from jax import lax
import numpy as np

D_MODEL = 1024
BATCH = 16
SEQ = 2048
DEPTH = 1
DEC_BATCH = 8
DEC_SEQ = 8192
PAST_LEN = 128

N_META = 16
ATTN_WIDTH = D_MODEL // 2
POOL_WIDTH = D_MODEL // 2
MIX_WIDTH = ATTN_WIDTH + POOL_WIDTH
N_ATTN_HEADS = 4
ATTN_DV = ATTN_WIDTH // N_ATTN_HEADS
ATTN_DQK = ATTN_DV // 2
QK_COLS = N_ATTN_HEADS * 2 * ATTN_DQK
POOL_WINDOWS = (2, 4, 8, 16)
N_POOL_GROUPS = len(POOL_WINDOWS)
POOL_GROUP = POOL_WIDTH // N_POOL_GROUPS
IN_COLS = 2 * QK_COLS + ATTN_WIDTH + POOL_WIDTH
D_FF = 2816
N_BUCKETS = 32
MAX_DISTANCE = 128
Q_BLOCK = 128
EPS = 1e-6

kernel_name = "hymba_diffattn_pool_macaron_encoder"


def _rmsnorm(x, g):
    xf = x.astype(jnp.float32)
    y = xf * lax.rsqrt(jnp.mean(xf * xf, axis=-1, keepdims=True) + EPS)
    return (y * g.astype(jnp.float32)).astype(x.dtype)


def _swiglu(x, w_gate, w_up, w_down):
    return (jax.nn.silu(x @ w_gate) * (x @ w_up)) @ w_down


def _rel_bucket(rel):
    half = N_BUCKETS // 2
    max_exact = half // 2
    ret = jnp.where(rel > 0, half, 0)
    n = jnp.abs(rel)
    nf = jnp.maximum(n, 1).astype(jnp.float32)
    large = max_exact + (jnp.log(nf / max_exact) / math.log(MAX_DISTANCE / max_exact)
                         * (half - max_exact)).astype(jnp.int32)
    large = jnp.minimum(large, half - 1)
    return ret + jnp.where(n < max_exact, n, large)


def _diff_attention(q, k, v, lam, bias_table):
    B, L = q.shape[0], q.shape[1]
    nb = -(-L // Q_BLOCK)
    Lp = nb * Q_BLOCK
    qp = jnp.pad(q, ((0, 0), (0, Lp - L), (0, 0), (0, 0), (0, 0)))
    qb = qp.reshape(B, nb, Q_BLOCK, N_ATTN_HEADS, 2, ATTN_DQK).transpose(1, 0, 2, 3, 4, 5)
    qpos = jnp.arange(Lp, dtype=jnp.int32).reshape(nb, Q_BLOCK)
    kpos = jnp.arange(L, dtype=jnp.int32)
    scale = ATTN_DQK ** -0.5

    def block(args):
        q_blk, pos = args
        bucket = _rel_bucket(kpos[None, :] - pos[:, None])
        bias = jnp.take(bias_table, bucket, axis=0).astype(jnp.float32)
        bias = bias.transpose(2, 0, 1)
        logits = jnp.einsum('bqhcd,bkhcd->bhcqk', q_blk, k).astype(jnp.float32) * scale
        logits = logits + bias[None, :, None]
        p = jax.nn.softmax(logits, axis=-1)
        a = p[:, :, 0] - lam * p[:, :, 1]
        return jnp.einsum('bhqk,bkhd->bqhd', a.astype(v.dtype), v)

    o = lax.map(block, (qb, qpos))
    o = o.transpose(1, 0, 2, 3, 4).reshape(B, Lp, N_ATTN_HEADS, ATTN_DV)
    return o[:, :L]


def _pool_mixer(u, pool_w, pool_scale):
    B, L = u.shape[0], u.shape[1]
    ug = u.reshape(B, L, N_POOL_GROUPS, POOL_GROUP)
    pos = jnp.arange(L, dtype=jnp.int32)
    outs = []
    for g, w in enumerate(POOL_WINDOWS):
        xg = ug[:, :, g].astype(jnp.float32)
        cs = jnp.concatenate([jnp.zeros((B, 1, POOL_GROUP), jnp.float32),
                              jnp.cumsum(xg, axis=1)], axis=1)
        lo = jnp.clip(pos - w // 2, 0, L)
        hi = jnp.clip(pos + w // 2, 0, L)
        cnt = (hi - lo).astype(jnp.float32)[None, :, None]
        mean = (jnp.take(cs, hi, axis=1) - jnp.take(cs, lo, axis=1)) / cnt
        outs.append((mean - xg).astype(u.dtype))
    pooled = jnp.stack(outs, axis=2)
    mixed = jnp.einsum('blgc,gcd->blgd', pooled, pool_w)
    return mixed.reshape(B, L, POOL_WIDTH) * pool_scale


def _encode(x, meta_tokens, rel_bias_table,
            norm_ffn1, ffn1_w_gate, ffn1_w_up, ffn1_w_down,
            norm_mix, w_in, lambda_q1, lambda_k1, lambda_q2, lambda_k2, subln_gain,
            pool_w, pool_scale, w_out,
            norm_ffn2, ffn2_w_gate, ffn2_w_up, ffn2_w_down, norm_final):
    B = x.shape[0]
    meta = jnp.broadcast_to(meta_tokens[None].astype(x.dtype), (B, N_META, D_MODEL))
    h = jnp.concatenate([meta, x], axis=1)
    L = h.shape[1]
    for l in range(DEPTH):
        lambda_init = 0.8 - 0.6 * math.exp(-0.3 * l)
        h = h + 0.5 * _swiglu(_rmsnorm(h, norm_ffn1[l]), ffn1_w_gate[l], ffn1_w_up[l], ffn1_w_down[l])
        u = _rmsnorm(h, norm_mix[l]) @ w_in[l]
        q = u[..., :QK_COLS].reshape(B, L, N_ATTN_HEADS, 2, ATTN_DQK)
        k = u[..., QK_COLS:2 * QK_COLS].reshape(B, L, N_ATTN_HEADS, 2, ATTN_DQK)
        v = u[..., 2 * QK_COLS:2 * QK_COLS + ATTN_WIDTH].reshape(B, L, N_ATTN_HEADS, ATTN_DV)
        p_in = u[..., 2 * QK_COLS + ATTN_WIDTH:]
        lam = (jnp.exp(jnp.sum(lambda_q1[l].astype(jnp.float32) * lambda_k1[l].astype(jnp.float32)))
               - jnp.exp(jnp.sum(lambda_q2[l].astype(jnp.float32) * lambda_k2[l].astype(jnp.float32)))
               + lambda_init)
        o = _diff_attention(q, k, v, lam, rel_bias_table)
        o = _rmsnorm(o, subln_gain[l]) * (1.0 - lambda_init)
        attn_out = o.reshape(B, L, ATTN_WIDTH)
        pool_out = _pool_mixer(p_in, pool_w[l], pool_scale[l])
        h = h + jnp.concatenate([attn_out, pool_out], axis=-1) @ w_out[l]
        h = h + 0.5 * _swiglu(_rmsnorm(h, norm_ffn2[l]), ffn2_w_gate[l], ffn2_w_up[l], ffn2_w_down[l])
    h = _rmsnorm(h, norm_final)
    return h[:, N_META:]


def setup_inputs(seed: int = 0) -> dict:
    key = jax.random.key(seed)
    ks = jax.random.split(key, 32)
    f32 = jnp.float32

    def nrm(k, shape, scale):
        return jax.random.normal(k, shape, f32) * scale

    def gain(k, shape):
        return 1.0 + 0.02 * jax.random.normal(k, shape, f32)

    return {
        "x_prompt": nrm(ks[0], (BATCH, SEQ, D_MODEL), 1.0),
        "x_sample": nrm(ks[1], (DEC_BATCH, DEC_SEQ, D_MODEL), 1.0),
        "meta_tokens": nrm(ks[2], (N_META, D_MODEL), 1.0),
        "rel_bias_table": nrm(ks[3], (N_BUCKETS, N_ATTN_HEADS), 0.5),
        "norm_ffn1": gain(ks[4], (DEPTH, D_MODEL)),
        "ffn1_w_gate": nrm(ks[5], (DEPTH, D_MODEL, D_FF), D_MODEL ** -0.5),
        "ffn1_w_up": nrm(ks[6], (DEPTH, D_MODEL, D_FF), D_MODEL ** -0.5),
        "ffn1_w_down": nrm(ks[7], (DEPTH, D_FF, D_MODEL), D_FF ** -0.5),
        "norm_mix": gain(ks[8], (DEPTH, D_MODEL)),
        "w_in": nrm(ks[9], (DEPTH, D_MODEL, IN_COLS), D_MODEL ** -0.5),
        "lambda_q1": nrm(ks[10], (DEPTH, ATTN_DQK), 0.1),
        "lambda_k1": nrm(ks[11], (DEPTH, ATTN_DQK), 0.1),
        "lambda_q2": nrm(ks[12], (DEPTH, ATTN_DQK), 0.1),
        "lambda_k2": nrm(ks[13], (DEPTH, ATTN_DQK), 0.1),
        "subln_gain": gain(ks[14], (DEPTH, ATTN_DV)),
        "pool_w": nrm(ks[15], (DEPTH, N_POOL_GROUPS, POOL_GROUP, POOL_GROUP), POOL_GROUP ** -0.5),
        "pool_scale": 1.0 + 0.1 * jax.random.normal(ks[16], (DEPTH, POOL_WIDTH), f32),
        "w_out": nrm(ks[17], (DEPTH, MIX_WIDTH, D_MODEL), MIX_WIDTH ** -0.5),
        "norm_ffn2": gain(ks[18], (DEPTH, D_MODEL)),
        "ffn2_w_gate": nrm(ks[19], (DEPTH, D_MODEL, D_FF), D_MODEL ** -0.5),
        "ffn2_w_up": nrm(ks[20], (DEPTH, D_MODEL, D_FF), D_MODEL ** -0.5),
        "ffn2_w_down": nrm(ks[21], (DEPTH, D_FF, D_MODEL), D_FF ** -0.5),
        "norm_final": gain(ks[22], (D_MODEL,)),
    }


def reference(x_prompt, x_sample, meta_tokens, rel_bias_table,
              norm_ffn1, ffn1_w_gate, ffn1_w_up, ffn1_w_down,
              norm_mix, w_in, lambda_q1, lambda_k1, lambda_q2, lambda_k2, subln_gain,
              pool_w, pool_scale, w_out,
              norm_ffn2, ffn2_w_gate, ffn2_w_up, ffn2_w_down, norm_final):
    y_prompt = _encode(x_prompt, meta_tokens, rel_bias_table,
                       norm_ffn1, ffn1_w_gate, ffn1_w_up, ffn1_w_down,
                       norm_mix, w_in, lambda_q1, lambda_k1, lambda_q2, lambda_k2, subln_gain,
                       pool_w, pool_scale, w_out,
                       norm_ffn2, ffn2_w_gate, ffn2_w_up, ffn2_w_down, norm_final)
    y_sample = _encode(x_sample, meta_tokens, rel_bias_table,
                       norm_ffn1, ffn1_w_gate, ffn1_w_up, ffn1_w_down,
                       norm_mix, w_in, lambda_q1, lambda_k1, lambda_q2, lambda_k2, subln_gain,
                       pool_w, pool_scale, w_out,
                       norm_ffn2, ffn2_w_gate, ffn2_w_up, ffn2_w_down, norm_final)
    return (y_prompt, y_sample)
```

```python
import math
from contextlib import ExitStack

import numpy as np
import ml_dtypes

import concourse.bass as bass
import concourse.mybir as mybir
from concourse.bass_utils import run_bass_kernel_spmd

F32 = mybir.dt.float32
BF16 = mybir.dt.bfloat16
AF = mybir.ActivationFunctionType
ALU = mybir.AluOpType

D = 1024
DC = 8
FF = 2816
FC = 22
T = 512
NMETA = 16
EPS = 1e-6
NH = 4
N0 = 640
NR = 1280
LAMBDA_INIT = 0.8 - 0.6 * math.exp(-0.3 * 0)
NS_RING = 4


class Buf:
    __slots__ = ("name", "w", "r")

    def __init__(self, name):
        self.name = name
        self.w = {}
        self.r = {}


class Sched:
    def __init__(self, nc, stack, n_dma=28):
        self.nc = nc
        self.eng = {"pe": nc.tensor, "act": nc.scalar, "dve": nc.vector, "pool": nc.gpsimd, "sp": nc.sync}
        self.semh = {}
        self.cnt = {}
        for k in self.eng:
            self.semh[k] = stack.enter_context(nc.semaphore("e_" + k))
            self.cnt[k] = 0
        self.nd = {"sp": n_dma - 12, "pool": 12}
        for q, n in self.nd.items():
            for i in range(n):
                self.semh[("d", q, i)] = stack.enter_context(nc.semaphore("d_%s_%d" % (q, i)))
                self.cnt[("d", q, i)] = 0
        self.dnext = {"sp": 0, "pool": 0}
        self.clock = {k: {} for k in self.eng}
        self.evclock = {}
        self.nops = 0

    def _merge(self, dst, src):
        for k, v in src.items():
            if dst.get(k, 0) < v:
                dst[k] = v

    def _wait(self, eng, need, skip_self_pe=True):
        e = self.eng[eng]
        ck = self.clock[eng]
        for k, v in need.items():
            if skip_self_pe and k == "pe" and eng == "pe":
                continue
            if ck.get(k, 0) >= v:
                continue
            e.wait_ge(self.semh[k], v)
            if ck.get(k, 0) < v:
                ck[k] = v
            ec = self.evclock.get((k, v))
            if ec is not None:
                self._merge(ck, ec)

    def op(self, eng, emit, reads=(), writes=(), dma=False):
        need = {}

        def add(d):
            for k, v in d.items():
                if need.get(k, 0) < v:
                    need[k] = v

        for b in reads:
            add(b.w)
        selfkey = "pe" if (eng == "pe" and not dma) else None
        for b in writes:
            for d in (b.w, b.r):
                for k, v in d.items():
                    if k != selfkey and need.get(k, 0) < v:
                        need[k] = v
        if dma:
            i = self.dnext[eng]
            self.dnext[eng] = (i + 1) % self.nd[eng]
            key = ("d", eng, i)
            if self.cnt[key] > 0 and need.get(key, 0) < self.cnt[key]:
                need[key] = self.cnt[key]
            inc = 16
        else:
            key = eng
            inc = 1
        self._wait(eng, need)
        self.cnt[key] += inc
        val = self.cnt[key]
        ins = emit(self.eng[eng])
        ins.then_inc(self.semh[key], inc)
        ec = dict(self.clock[eng])
        ec[key] = val
        if eng == "pe" and not dma:
            self.clock["pe"]["pe"] = val - 1
            ec["pe"] = val
        self.evclock[(key, val)] = ec
        for b in writes:
            b.w = {key: val}
            b.r = {}
        for b in reads:
            if b.r.get(key, 0) < val:
                b.r[key] = val
        self.nops += 1
        return {key: val}

    def barrier(self):
        allev = {k: v for k, v in self.cnt.items() if v > 0}
        for eng in self.eng:
            self._wait(eng, allev, skip_self_pe=False)

    def final_wait(self, eng="sp"):
        allev = {k: v for k, v in self.cnt.items() if v > 0}
        self._wait(eng, allev, skip_self_pe=False)


def _rel_bucket_np(rel):
    half = 16
    max_exact = 8
    ret = np.where(rel > 0, half, 0)
    n = np.abs(rel)
    nf = np.maximum(n, 1).astype(np.float32)
    large = max_exact + (np.log(nf / np.float32(max_exact)) / np.float32(math.log(128 / max_exact))
                         * np.float32(half - max_exact)).astype(np.int32)
    large = np.minimum(large, half - 1)
    return ret + np.where(n < max_exact, n, large)


def host_consts():
    rel = N0 - np.arange(NR, dtype=np.int64)
    bk = _rel_bucket_np(rel)
    erev = np.zeros((32, NR), np.float32)
    erev[bk, np.arange(NR)] = 1.0
    ident = np.eye(128, dtype=np.float32).astype(ml_dtypes.bfloat16)
    jflip = np.ascontiguousarray(np.eye(128, dtype=np.float32)[::-1])
    return {"erevT": erev, "ident": ident, "jflip": jflip}


def build(seqs):
    nc = bass.Bass("TRN2", target_bir_lowering=False)
    NTOK = sum(seqs)
    NBLK = NTOK // 128
    toff = [sum(seqs[:i]) for i in range(len(seqs))]
    pcol = [sum(s + 16 for s in seqs[:i]) for i in range(len(seqs))]
    PW = sum(s + 16 for s in seqs)
    SMAX = max(seqs)

    def din(name, shape, dt=F32):
        return nc.dram_tensor(name, list(shape), dt, kind="ExternalInput")

    x = din("x", [NTOK, D])
    meta = din("meta", [NMETA, D])
    table = din("table", [32, NH])
    g1 = din("g1", [D]); gmix = din("gmix", [D]); g2 = din("g2", [D]); gfin = din("gfin", [D])
    wg1 = din("wg1", [D, FF]); wu1 = din("wu1", [D, FF]); wd1 = din("wd1", [FF, D])
    wg2 = din("wg2", [D, FF]); wu2 = din("wu2", [D, FF]); wd2 = din("wd2", [FF, D])
    win = din("win", [D, 2048]); wout = din("wout", [D, D])
    lq1 = din("lq1", [64]); lk1 = din("lk1", [64]); lq2 = din("lq2", [64]); lk2 = din("lk2", [64])
    subln = din("subln", [128]); poolw = din("poolw", [4, 128, 128]); pscale = din("pscale", [512])
    erevT = din("erevT", [32, NR]); ident_d = din("ident", [128, 128], BF16); jflip_d = din("jflip", [128, 128])
    y = nc.dram_tensor("y", [NTOK, D], F32, kind="ExternalOutput")

    def dscr(name, shape, dt):
        return nc.dram_tensor(name, list(shape), dt)

    wgu_d = [dscr("wgu%d" % i, [FC, 128, 2048], BF16) for i in (1, 2)]
    wd_d = [dscr("wdb%d" % i, [128, FC * D], BF16) for i in (1, 2)]
    win_d = dscr("winb", [128, DC * 2048], BF16)
    wout_d = dscr("woutb", [128, DC * D], BF16)
    poolw_d = dscr("poolwb", [128, 4 * 128], BF16)
    h1_d = dscr("h1d", [NTOK, D], F32)
    qT_d = dscr("qTd", [NH, 128, NTOK], BF16)
    kT_d = dscr("kTd", [NH, 128, NTOK], BF16)
    ao_d = dscr("aoTd", [NH, 128, NTOK], BF16)
    v_d = dscr("vd", [NH, 128, NBLK * 129], BF16)
    pT_d = dscr("pTd", [4, 128, PW], F32)
    kTm_d = dscr("kTm", [NH, 128, NMETA], BF16)
    vm_d = dscr("vmd", [NMETA, NH * 129], BF16)
    tr_d = dscr("trd", [NH, NR], F32)
    bias_d = dscr("biasd", [NH, 128, 6 * 512], F32)
    biasm_d = dscr("biasmd", [NH, NMETA, 512], F32)

    with ExitStack() as gs:
        S = Sched(nc, gs)
        op = S.op

        uid = [0]

        def sb(stack, name, shape, dt):
            uid[0] += 1
            return stack.enter_context(nc.sbuf_tensor("sb%d_%s" % (uid[0], name), list(shape), dt))

        pall = gs.enter_context(nc.psum_tensor("pall", [128, 8, 512], F32))
        banks = [pall[:, i, :] for i in range(8)]
        bankB = [Buf("bank%d" % i) for i in range(8)]

        ident = sb(gs, "ident", [128, 128], BF16)
        identB = Buf("ident")
        op("sp", lambda e: e.dma_start(out=ident[:], in_=ident_d.ap()), writes=[identB], dma=True)
        gT = sb(gs, "gT", [128, 3, DC], F32)
        gTB = Buf("gT")
        for k, g in enumerate((g1, gmix, g2)):
            op("sp", lambda e, k=k, g=g: e.dma_start(out=gT[:, k, :], in_=g.ap().rearrange("(c p) -> p c", p=128),
                                                    allow_slow_non_contiguous=True),
               writes=[gTB], dma=True)
        scT = sb(gs, "scT", [128, DC], F32)
        scTB = Buf("scT")
        sublT = sb(gs, "sublT", [128, 1], F32)
        sublB = Buf("sublT")
        op("sp", lambda e: e.dma_start(out=sublT[:], in_=subln.ap().rearrange("(p o) -> p o", o=1)),
           writes=[sublB], dma=True)
        op("sp", lambda e: e.dma_start(out=scT[:, 4:8], in_=pscale.ap().rearrange("(g p) -> p g", p=128),
                                       allow_slow_non_contiguous=True), writes=[scTB], dma=True)
        for c in range(4):
            op("dve", lambda e, c=c: e.tensor_scalar(out=scT[:, c:c + 1], in0=sublT[:], scalar1=1.0 - LAMBDA_INIT,
                                                     scalar2=None, op0=ALU.mult),
               reads=[sublB], writes=[scTB])
        gfin_bc = sb(gs, "gfin_bc", [128, D], F32)
        gfinB = Buf("gfin")
        op("sp", lambda e: e.dma_start(out=gfin_bc[:], in_=gfin.ap().rearrange("(o d) -> o d", o=1).broadcast_to([128, D])),
           writes=[gfinB], dma=True)
        lam4 = sb(gs, "lam4", [128, 4, 64], F32)
        lamB = Buf("lam4")
        for k, l in enumerate((lq1, lk1, lq2, lk2)):
            op("sp", lambda e, k=k, l=l: e.dma_start(out=lam4[:, k, :],
                                                    in_=l.ap().rearrange("(o d) -> o d", o=1).broadcast_to([128, 64])),
               writes=[lamB], dma=True)
        cb = sb(gs, "cb", [128, 2, NH], F32)
        cbB = Buf("cb")
        for k, row in enumerate((15, 31)):
            op("sp", lambda e, k=k, row=row: e.dma_start(out=cb[:, k, :], in_=table.ap()[row:row + 1, :].broadcast_to([128, NH])),
               writes=[cbB], dma=True)
        neglam = sb(gs, "neglam", [128, 1], F32)
        neglamB = Buf("neglam")
        ones_col = sb(gs, "ones_col", [128, 1], F32)
        zeros8 = sb(gs, "zeros8", [128, 4, 8], F32)
        zB = Buf("zeros")
        eps_t = sb(gs, "eps_t", [128, 1], F32)
        op("dve", lambda e: e.memset(eps_t[:], EPS), writes=[zB])
        op("dve", lambda e: e.memset(ones_col[:], 1.0), writes=[zB])
        op("dve", lambda e: e.memset(zeros8[:], 0.0), writes=[zB])

        wB = {k: Buf(k) for k in ("wgu1", "wgu2", "wd1", "wd2", "win", "wout", "poolw")}
        alt = [0]

        def conv(out_ap, in_ap, scale_ap, rB, wBf, dve_only=False):
            alt[0] ^= 1
            if alt[0] or dve_only:
                if scale_ap is None:
                    op("dve", lambda e: e.tensor_copy(out=out_ap, in_=in_ap), reads=rB, writes=wBf)
                else:
                    op("dve", lambda e: e.tensor_scalar(out=out_ap, in0=in_ap, scalar1=scale_ap, scalar2=None,
                                                        op0=ALU.mult), reads=rB, writes=wBf)
            else:
                if scale_ap is None:
                    op("act", lambda e: e.activation(out=out_ap, in_=in_ap, func=AF.Copy), reads=rB, writes=wBf)
                else:
                    op("act", lambda e: e.activation(out=out_ap, in_=in_ap, func=AF.Copy, scale=scale_ap),
                       reads=rB, writes=wBf)

        def gu_tasks(li, j0, s32, s32B, s16, s16B, dve_only):
            wg, wu, gk = ((wg1, wu1, 0), (wg2, wu2, 2))[li]
            nj = min(4, FC - j0)
            v32 = s32[:].rearrange("p (w c n) -> p w c n", w=2, c=DC)
            v16 = s16[:].rearrange("p (j w c n) -> p j w c n", j=4, w=2, c=DC)

            def ld():
                for w_i, wsrc in enumerate((wg, wu)):
                    op("sp", lambda e, wsrc=wsrc, w_i=w_i: e.dma_start(
                        out=v32[:, w_i, :, 0:nj * 128],
                        in_=wsrc.ap()[:, j0 * 128:(j0 + nj) * 128].rearrange("(c p) n -> p c n", p=128)),
                       writes=[s32B], dma=True)
            tasks = [ld]
            for w_i in range(2):
                for c0 in range(0, DC, 2):
                    def cv(w_i=w_i, c0=c0):
                        for c in (c0, c0 + 1):
                            conv(v16[:, 0:nj, w_i, c, :], v32[:, w_i, c, 0:nj * 128].rearrange("p (j n) -> p j n", j=nj),
                                 gT[:, gk, c:c + 1], [s32B, gTB], [s16B], dve_only)
                    tasks.append(cv)

            def st_():
                op("pool", lambda e: e.dma_start(
                    out=wgu_d[li].ap()[j0:j0 + nj].rearrange("j p n -> p j n"),
                    in_=s16[:, 0:nj * 2048].rearrange("p (j n) -> p j n", j=nj)),
                   reads=[s16B], writes=[wB["wgu%d" % (li + 1)]], dma=True)
            tasks.append(st_)
            return tasks

        def wd_tasks(li, j2, s32, s32B, s16, s16B, dve_only):
            wd = (wd1, wd2)[li]

            def ld():
                op("sp", lambda e: e.dma_start(
                    out=s32[:, 0:2048].rearrange("p (j n) -> p j n", j=2),
                    in_=wd.ap()[j2 * 256:(j2 + 1) * 256, :].rearrange("(j p) n -> p j n", p=128)),
                   writes=[s32B], dma=True)

            def cv():
                conv(s16[:, 0:1024], s32[:, 0:1024], None, [s32B], [s16B], dve_only)
                conv(s16[:, 1024:2048], s32[:, 1024:2048], None, [s32B], [s16B], dve_only)

            def st_():
                op("pool", lambda e: e.dma_start(out=wd_d[li].ap()[:, j2 * 2048:(j2 + 1) * 2048], in_=s16[:, 0:2048]),
                   reads=[s16B], writes=[wB["wd%d" % (li + 1)]], dma=True)
            return [ld, cv, st_]

        def win_tasks(c, s32, s32B, s16, s16B, dve_only):
            def ld():
                op("sp", lambda e: e.dma_start(out=s32[:, 0:2048], in_=win.ap()[c * 128:(c + 1) * 128, :]),
                   writes=[s32B], dma=True)

            def cv():
                conv(s16[:, 0:1024], s32[:, 0:1024], gT[:, 1, c:c + 1], [s32B, gTB], [s16B], dve_only)
                conv(s16[:, 1024:2048], s32[:, 1024:2048], gT[:, 1, c:c + 1], [s32B, gTB], [s16B], dve_only)

            def st_():
                op("pool", lambda e: e.dma_start(out=win_d.ap()[:, c * 2048:(c + 1) * 2048], in_=s16[:, 0:2048]),
                   reads=[s16B], writes=[wB["win"]], dma=True)
            return [ld, cv, st_]

        def wout_tasks(c2, s32, s32B, s16, s16B, dve_only):
            def ld():
                op("sp", lambda e: e.dma_start(
                    out=s32[:, 0:2048].rearrange("p (c n) -> p c n", c=2),
                    in_=wout.ap()[c2 * 256:(c2 + 1) * 256, :].rearrange("(c p) n -> p c n", p=128)),
                   writes=[s32B], dma=True)

            def cv():
                for cc in range(2):
                    c = c2 * 2 + cc
                    conv(s16[:, cc * 1024:(cc + 1) * 1024], s32[:, cc * 1024:(cc + 1) * 1024], scT[:, c:c + 1],
                         [s32B, scTB], [s16B], dve_only)

            def st_():
                op("pool", lambda e: e.dma_start(out=wout_d.ap()[:, c2 * 2048:(c2 + 1) * 2048], in_=s16[:, 0:2048]),
                   reads=[s16B], writes=[wB["wout"]], dma=True)
            return [ld, cv, st_]

        def poolw_tasks(s32, s32B, s16, s16B, dve_only):
            def ld():
                op("sp", lambda e: e.dma_start(out=s32[:, 0:512].rearrange("p (g d) -> p g d", g=4),
                                               in_=poolw.ap().rearrange("g c d -> c g d")),
                   writes=[s32B], dma=True)

            def cv():
                conv(s16[:, 0:512], s32[:, 0:512], None, [s32B], [s16B], dve_only)

            def st_():
                op("pool", lambda e: e.dma_start(out=poolw_d.ap(), in_=s16[:, 0:512]),
                   reads=[s16B], writes=[wB["poolw"]], dma=True)
            return [ld, cv, st_]

        with ExitStack() as ps:
            st32 = [sb(ps, "st32_%d" % i, [128, 2048], F32) for i in range(3)]
            st32B = [Buf("st32_%d" % i) for i in range(3)]
            st16 = [sb(ps, "st16_%d" % i, [128, 2048], BF16) for i in range(3)]
            st16B = [Buf("st16_%d" % i) for i in range(3)]
            g32 = [sb(ps, "g32_%d" % i, [128, 2 * DC * 512], F32) for i in range(2)]
            g32B = [Buf("g32_%d" % i) for i in range(2)]
            g16 = [sb(ps, "g16_%d" % i, [128, 4 * 2048], BF16) for i in range(2)]
            g16B = [Buf("g16_%d" % i) for i in range(2)]
            gi = 0
            for j0 in range(0, FC, 4):
                i = gi % 2
                gi += 1
                for f in gu_tasks(0, j0, g32[i], g32B[i], g16[i], g16B[i], False):
                    f()
            cnt = 0
            for j2 in range(FC // 2):
                i = cnt % 3
                cnt += 1
                for f in wd_tasks(0, j2, st32[i], st32B[i], st16[i], st16B[i], False):
                    f()
            for c in range(DC):
                i = cnt % 3
                cnt += 1
                for f in win_tasks(c, st32[i], st32B[i], st16[i], st16B[i], False):
                    f()

            tab_sb = sb(ps, "tab_sb", [32, NH], F32)
            er_sb = sb(ps, "er_sb", [32, NR], F32)
            tr_sb = sb(ps, "tr_sb", [NH, NR], F32)
            tabB, erB, trB, trdB = Buf("tab"), Buf("er"), Buf("tr"), Buf("trd")
            biasdB = Buf("biasd")
            op("sp", lambda e: e.dma_start(out=tab_sb[:], in_=table.ap()), writes=[tabB], dma=True)
            op("sp", lambda e: e.dma_start(out=er_sb[:], in_=erevT.ap()), writes=[erB], dma=True)
            for k3 in range(3):
                n0 = k3 * 512
                n1 = min(NR, n0 + 512)
                op("pe", lambda e, n0=n0, n1=n1, k3=k3: e.matmul(banks[k3][0:NH, 0:n1 - n0], lhsT=tab_sb[:, :],
                                                                 rhs=er_sb[:, n0:n1], start=True, stop=True),
                   reads=[tabB, erB], writes=[bankB[k3]])
                op("dve", lambda e, n0=n0, n1=n1, k3=k3: e.tensor_copy(out=tr_sb[:, n0:n1], in_=banks[k3][0:NH, 0:n1 - n0]),
                   reads=[bankB[k3]], writes=[trB])
            op("pool", lambda e: e.dma_start(out=tr_d.ap(), in_=tr_sb[:]), reads=[trB], writes=[trdB], dma=True)
            jf = sb(ps, "jf", [128, 128], F32); jfB = Buf("jf")
            op("sp", lambda e: e.dma_start(out=jf[:], in_=jflip_d.ap()), writes=[jfB], dma=True)
            hk = [sb(ps, "hk%d" % i, [128, 512], F32) for i in range(2)]; hkB = [Buf("hk%d" % i) for i in range(2)]
            bo = [sb(ps, "bo%d" % i, [128, 512], F32) for i in range(2)]; boB = [Buf("bo%d" % i) for i in range(2)]
            bi = 0
            for hh in range(NH):
                for dd in range(7):
                    i2 = bi % 2
                    bi += 1
                    if dd < 6:
                        npart = 128
                        base = N0 - 128 * (dd - 1) - 127
                        lhs = jf[:, :]
                    else:
                        npart = NMETA
                        base = N0 + 1
                        lhs = jf[0:NMETA, 112:128]
                    src = bass.AP(tr_d, hh * NR + base, [[1, npart], [1, 512]])
                    op("sp", lambda e: e.dma_start(out=hk[i2][0:npart, :], in_=src), reads=[trdB], writes=[hkB[i2]], dma=True)
                    bk = 4 + i2
                    op("pe", lambda e: e.matmul(banks[bk][0:npart, :], lhsT=lhs, rhs=hk[i2][0:npart, :], start=True, stop=True),
                       reads=[jfB, hkB[i2]], writes=[bankB[bk]])
                    op("dve", lambda e: e.tensor_scalar(out=bo[i2][0:npart, :], in0=banks[bk][0:npart, :], scalar1=8.0,
                                                        scalar2=None, op0=ALU.mult),
                       reads=[bankB[bk]], writes=[boB[i2]])
                    if dd < 6:
                        op("pool", lambda e: e.dma_start(out=bias_d.ap()[hh][:, dd * 512:(dd + 1) * 512], in_=bo[i2][:, :]),
                           reads=[boB[i2]], writes=[biasdB], dma=True)
                    else:
                        op("pool", lambda e: e.dma_start(out=biasm_d.ap()[hh], in_=bo[i2][0:NMETA, :]),
                           reads=[boB[i2]], writes=[biasdB], dma=True)
        S.barrier()

        def emit_rstd(ctx, nb, n_feat):
            op("act", lambda e: e.activation(out=ctx.lnv[:, 0:nb], in_=ctx.ssq[:, 0:nb], func=AF.Ln,
                                             bias=eps_t[:, 0:1], scale=1.0 / n_feat),
               reads=ctx.ssqB[0:nb] + [zB], writes=[ctx.lnvB])
            op("act", lambda e: e.activation(out=ctx.rstd[:, 0:nb], in_=ctx.lnv[:, 0:nb], func=AF.Exp, scale=-0.5),
               reads=[ctx.lnvB], writes=ctx.rstdB[0:nb])

        def emit_norm_act1(ctx, h, hB, nb):
            for b in range(nb):
                op("act", lambda e, b=b: e.activation(out=ctx.junk[:], in_=h[:, b, :], func=AF.Square,
                                                      accum_out=ctx.ssq[:, b:b + 1]),
                   reads=[hB[b]], writes=[ctx.junkB, ctx.ssqB[b]])
            emit_rstd(ctx, nb, D)

        def emit_norm_act(ctx, h, hB, nb):
            emit_norm_act1(ctx, h, hB, nb)
            emit_norm_act2(ctx, h, hB, nb)

        def emit_norm_act2(ctx, h, hB, nb):
            for b in range(nb):
                op("act", lambda e, b=b: e.activation(out=ctx.xn[b][:], in_=h[:, b, :], func=AF.Copy,
                                                      scale=ctx.rstd[:, b:b + 1]),
                   reads=[hB[b], ctx.rstdB[b]], writes=[ctx.xnB[b]])

        def emit_norm_pe(ctx, nb, xnT, xnTB):
            for b in range(nb):
                tb = 6 + (b % 2)
                tbank = banks[tb][:].bitcast(BF16)

                def tr(e, b=b, tbank=tbank):
                    ins = None
                    for c in range(DC):
                        ins = e.transpose(tbank[:, c * 128:(c + 1) * 128], ctx.xn[b][:, c * 128:(c + 1) * 128], ident[:])
                    return ins
                op("pe", tr, reads=[ctx.xnB[b], identB], writes=[bankB[tb]])
                op("dve", lambda e, b=b, tbank=tbank: e.tensor_copy(
                    out=xnT[:, :, b * 128:(b + 1) * 128], in_=tbank.rearrange("p (c n) -> p c n", c=DC)),
                   reads=[bankB[tb]], writes=[xnTB[b]])

        class MMPool:
            def __init__(self):
                self.i = 0

            def get(self):
                i = self.i
                self.i = (i + 1) % 6
                return banks[i], bankB[i]
        mmp = MMPool()

        class WStream:
            def __init__(self, stack, chunks):
                self.chunks = chunks
                self.slots = [sb(stack, "wslot%d" % i, [128, 2048], BF16) for i in range(NS_RING)]
                self.slotB = [Buf("wslot%d" % i) for i in range(NS_RING)]
                self.loaded = 0
                self.used = 0

            def _load(self):
                if self.loaded >= len(self.chunks):
                    return
                t, j, srcB = self.chunks[self.loaded]
                i = self.loaded % NS_RING
                op("sp", lambda e: e.dma_start(out=self.slots[i][:], in_=t.ap()[j]),
                   reads=[srcB], writes=[self.slotB[i]], dma=True)
                self.loaded += 1

            def prime(self):
                for _ in range(NS_RING - 1):
                    self._load()

            def get(self):
                self._load()
                i = self.used % NS_RING
                self.used += 1
                return self.slots[i], self.slotB[i]

        def load_wd(ctx, li):
            jr = [(0, 6), (6, 12), (12, 17), (17, 22)]
            for k, (j0, j1) in enumerate(jr):
                op("sp", lambda e, j0=j0, j1=j1: e.dma_start(out=ctx.wd[:, j0 * D:j1 * D], in_=wd_d[li].ap()[:, j0 * D:j1 * D]),
                   reads=[wB["wd%d" % (li + 1)]], writes=[ctx.wdB[k]], dma=True)

        def emit_gu(ctx, nb, N, ws, xnT, xnTB, j0, j1):
            xr = [xnTB[b] for b in range(nb)]
            for j in range(j0, j1):
                slot, slotB = ws.get()
                pg, pgB = mmp.get()
                pu, puB = mmp.get()

                def mm2(e, pg=pg, pu=pu, slot=slot):
                    ins = None
                    for w_i, dst in ((0, pg), (1, pu)):
                        for c in range(DC):
                            o = (w_i * DC + c) * 128
                            ins = e.matmul(dst[:, 0:N], lhsT=slot[:, o:o + 128], rhs=xnT[:, c, 0:N],
                                           start=(c == 0), stop=(c == DC - 1))
                    return ins
                op("pe", mm2, reads=[slotB] + xr, writes=[pgB, puB])
                si = j % 2
                op("act", lambda e, pg=pg, si=si: e.activation(out=ctx.sg[si][:, 0:N], in_=pg[:, 0:N], func=AF.Silu),
                   reads=[pgB], writes=[ctx.sgB[si]])
                op("dve", lambda e, pu=pu, si=si, j=j: e.tensor_tensor(out=ctx.aT[:, j, 0:N], in0=pu[:, 0:N],
                                                                         in1=ctx.sg[si][:, 0:N], op=ALU.mult),
                   reads=[puB, ctx.sgB[si]], writes=[ctx.aTB[j]])

        def emit_dn(ctx, h, hB, nb):
            for b in range(nb):
                for half in range(2):
                    po, poB = mmp.get()

                    def mmd(e, po=po, b=b, half=half):
                        ins = None
                        for j in range(FC):
                            ins = e.matmul(po[:, :], lhsT=ctx.aT[:, j, b * 128:(b + 1) * 128],
                                           rhs=ctx.wd[:, j * D + half * 512:j * D + (half + 1) * 512],
                                           start=(j == 0), stop=(j == FC - 1))
                        return ins
                    op("pe", mmd, reads=ctx.aTB + ctx.wdB, writes=[poB])
                    op("dve", lambda e, po=po, b=b, half=half: e.scalar_tensor_tensor(
                        out=h[:, b, half * 512:(half + 1) * 512], in0=po[:, :], scalar=0.5,
                        in1=h[:, b, half * 512:(half + 1) * 512], op0=ALU.mult, op1=ALU.add),
                       reads=[poB, hB[b]], writes=[hB[b]])

        class Ctx:
            pass

        def run_gu(gu, hooks):
            for j in range(FC):
                gu(j, j + 1)
                for f in hooks.get(j, ()):
                    f()

        def common_ctx(stack, n_xnT):
            ctx = Ctx()
            ctx.junk = sb(stack, "junk", [128, D], BF16); ctx.junkB = Buf("junk")
            ctx.ssq = sb(stack, "ssq", [128, 4], F32); ctx.ssqB = [Buf("ssq%d" % b) for b in range(4)]
            ctx.rstd = sb(stack, "rstd", [128, 4], F32); ctx.rstdB = [Buf("rstd%d" % b) for b in range(4)]
            ctx.lnv = sb(stack, "lnv", [128, 4], F32); ctx.lnvB = Buf("lnv")
            ctx.xn = [sb(stack, "xn%d" % i, [128, D], BF16) for i in range(4)]
            ctx.xnB = [Buf("xn%d" % i) for i in range(4)]
            ctx.xnT = [sb(stack, "xnT%d" % k, [128, DC, T], BF16) for k in range(n_xnT)]
            ctx.xnTB = [[Buf("xnT%d_%d" % (k, b)) for b in range(4)] for k in range(n_xnT)]
            ctx.aT = sb(stack, "aT", [128, FC, T], BF16); ctx.aTB = [Buf("aT%d" % j) for j in range(FC)]
            ctx.sg = [sb(stack, "sg%d" % i, [128, T], BF16) for i in range(2)]
            ctx.sgB = [Buf("sg%d" % i) for i in range(2)]
            ctx.wd = sb(stack, "wd", [128, FC * D], BF16); ctx.wdB = [Buf("wd%d" % k) for k in range(4)]
            ctx.hbuf = [sb(stack, "h%d" % i, [128, 4, D], F32) for i in range(3)]
            ctx.hB = [[Buf("h%d_%d" % (i, b)) for b in range(4)] for i in range(3)]
            return ctx

        tiles = []
        for s, Sl in enumerate(seqs):
            for t in range(Sl // T):
                tiles.append((s, t))
        nblk_s = [Sl // 128 for Sl in seqs]
        blk0_s = [toff[s] // 128 for s in range(len(seqs))]

        h1B = {tl: Buf("h1_%d_%d" % tl) for tl in tiles}
        qB = {tl: Buf("q_%d_%d" % tl) for tl in tiles}
        kB = {tl: Buf("k_%d_%d" % tl) for tl in tiles}
        vB = {tl: Buf("v_%d_%d" % tl) for tl in tiles}
        pB = {tl: Buf("p_%d_%d" % tl) for tl in tiles}
        aoB = {tl: Buf("ao_%d_%d" % tl) for tl in tiles}
        metaKB, metaVB = Buf("kTm"), Buf("vm")
        pEdgeB = [Buf("pedge%d" % s) for s in range(len(seqs))]

        with ExitStack() as ps:
            ctx = common_ctx(ps, 2)
            winS = sb(ps, "winS", [128, DC * 2048], BF16); winSB = Buf("winS")
            stq = [sb(ps, "stq%d" % i, [128, T], BF16) for i in range(2)]; stqB = [Buf("stq%d" % i) for i in range(2)]
            stp = [sb(ps, "stp%d" % i, [128, T], F32) for i in range(2)]; stpB = [Buf("stp%d" % i) for i in range(2)]
            vst = sb(ps, "vst", [128, 4, NH, 129], BF16); vstB = Buf("vst")
            op("dve", lambda e: e.memset(vst[:], 1.0), writes=[vstB])
            jobs = [("meta", None)] + [("tile", tl) for tl in tiles]
            ws = WStream(ps, [(wgu_d[0], j, wB["wgu1"]) for _ in jobs for j in range(FC)])
            ws.prime()
            for s, Sl in enumerate(seqs):
                op("pool", lambda e, s=s, Sl=Sl: e.dma_start(
                    out=pT_d.ap()[:, :, pcol[s] + 8 + Sl:pcol[s] + 16 + Sl].rearrange("g p n -> p g n"), in_=zeros8[:]),
                   reads=[zB], writes=[pEdgeB[s]], dma=True)

            J = len(jobs)

            def jinfo(ji):
                kind, tl = jobs[ji]
                nb = 1 if kind == "meta" else 4
                return kind, tl, ctx.hbuf[ji % 3], ctx.hB[ji % 3], nb, nb * 128

            def load_x(ji):
                kind, tl, h, hB, nb, N = jinfo(ji)
                if kind == "meta":
                    op("dve", lambda e: e.memset(h[:, 0, :], 0.0), writes=[hB[0]])
                    op("sp", lambda e: e.dma_start(out=h[0:NMETA, 0, :], in_=meta.ap()), writes=[hB[0]], dma=True)
                else:
                    s, t = tl
                    tok0 = toff[s] + t * T
                    op("sp", lambda e: e.dma_start(out=h[:], in_=x.ap()[tok0:tok0 + T, :].rearrange("(b p) d -> p b d", p=128)),
                       writes=hB, dma=True)

            def N1(ji):
                kind, tl, h, hB, nb, N = jinfo(ji)
                emit_norm_act(ctx, h, hB, nb)
                emit_norm_pe(ctx, nb, ctx.xnT[0], ctx.xnTB[0])

            def GU(ji, j0, j1):
                kind, tl, h, hB, nb, N = jinfo(ji)
                emit_gu(ctx, nb, N, ws, ctx.xnT[0], ctx.xnTB[0], j0, j1)

            def DN(ji):
                kind, tl, h, hB, nb, N = jinfo(ji)
                emit_dn(ctx, h, hB, nb)
                if kind == "tile":
                    s, t = tl
                    tok0 = toff[s] + t * T
                    op("pool", lambda e: e.dma_start(
                        out=h1_d.ap()[tok0:tok0 + T, :].rearrange("(b p) d -> p b d", p=128), in_=h[:]),
                       reads=hB, writes=[h1B[tl]], dma=True)

            def NMact(ji):
                kind, tl, h, hB, nb, N = jinfo(ji)
                emit_norm_act(ctx, h, hB, nb)

            def NMpe(ji):
                kind, tl, h, hB, nb, N = jinfo(ji)
                emit_norm_pe(ctx, nb, ctx.xnT[1], ctx.xnTB[1])

            qi = [0]

            def WI(ji):
                kind, tl, h, hB, nb, N = jinfo(ji)
                xnT = ctx.xnT[1]
                xr = [ctx.xnTB[1][b] for b in range(nb)]
                if kind == "tile":
                    s, t = tl
                    tok0 = toff[s] + t * T
                for j in range(8):
                    if kind == "meta" and j < 4:
                        continue
                    pq, pqB = mmp.get()

                    def mmq(e, pq=pq, j=j):
                        ins = None
                        for c in range(DC):
                            ins = e.matmul(pq[:, 0:N], lhsT=winS[:, c * 2048 + j * 128:c * 2048 + (j + 1) * 128],
                                           rhs=xnT[:, c, 0:N], start=(c == 0), stop=(c == DC - 1))
                        return ins
                    op("pe", mmq, reads=[winSB] + xr, writes=[pqB])
                    k2 = qi[0] % 2
                    qi[0] += 1
                    if k2:
                        op("act", lambda e, pq=pq, k2=k2: e.activation(out=stq[k2][:, 0:N], in_=pq[:, 0:N], func=AF.Copy),
                           reads=[pqB], writes=[stqB[k2]])
                    else:
                        op("dve", lambda e, pq=pq, k2=k2: e.tensor_copy(out=stq[k2][:, 0:N], in_=pq[:, 0:N]),
                           reads=[pqB], writes=[stqB[k2]])
                    if kind == "meta":
                        op("pool", lambda e, k2=k2, j=j: e.dma_start(out=kTm_d.ap()[j - 4], in_=stq[k2][:, 0:NMETA]),
                           reads=[stqB[k2]], writes=[metaKB], dma=True)
                    else:
                        dst = qT_d if j < 4 else kT_d
                        dB = qB[tl] if j < 4 else kB[tl]
                        op("pool", lambda e, k2=k2, j=j, dst=dst: e.dma_start(
                            out=dst.ap()[j % 4][:, tok0:tok0 + T], in_=stq[k2][:, :]),
                           reads=[stqB[k2]], writes=[dB], dma=True)
                for g in range(4):
                    pq, pqB = mmp.get()

                    def mmp_(e, pq=pq, g=g):
                        ins = None
                        for c in range(DC):
                            o = c * 2048 + 1536 + g * 128
                            ins = e.matmul(pq[:, 0:N], lhsT=winS[:, o:o + 128], rhs=xnT[:, c, 0:N],
                                           start=(c == 0), stop=(c == DC - 1))
                        return ins
                    op("pe", mmp_, reads=[winSB] + xr, writes=[pqB])
                    k2 = g % 2
                    op("act", lambda e, pq=pq, k2=k2: e.activation(out=stp[k2][:, 0:N], in_=pq[:, 0:N], func=AF.Copy),
                       reads=[pqB], writes=[stpB[k2]])
                    if kind == "meta":
                        for s2 in range(len(seqs)):
                            op("pool", lambda e, k2=k2, g=g, s2=s2: e.dma_start(
                                out=pT_d.ap()[g][:, pcol[s2]:pcol[s2] + 8], in_=stp[k2][:, 8:16]),
                               reads=[stpB[k2]], writes=[pEdgeB[s2]], dma=True)
                    else:
                        c0 = pcol[s] + 8 + t * T
                        op("pool", lambda e, k2=k2, g=g, c0=c0: e.dma_start(out=pT_d.ap()[g][:, c0:c0 + T], in_=stp[k2][:, :]),
                           reads=[stpB[k2]], writes=[pB[tl]], dma=True)
                for b in range(nb):
                    pv, pvB = mmp.get()

                    def mmv(e, pv=pv, b=b):
                        ins = None
                        for c in range(DC):
                            ins = e.matmul(pv[:, :], lhsT=xnT[:, c, b * 128:(b + 1) * 128],
                                           rhs=winS[:, c * 2048 + 1024:c * 2048 + 1536], start=(c == 0), stop=(c == DC - 1))
                        return ins
                    op("pe", mmv, reads=[winSB, ctx.xnTB[1][b]], writes=[pvB])
                    op("dve", lambda e, pv=pv, b=b: e.tensor_copy(out=vst[:, b, :, 0:128],
                                                                 in_=pv[:, :].rearrange("p (h d) -> p h d", h=NH)),
                       reads=[pvB], writes=[vstB])
                if kind == "meta":
                    op("pool", lambda e: e.dma_start(out=vm_d.ap(), in_=vst[0:NMETA, 0, :, :].rearrange("p h d -> p (h d)")),
                       reads=[vstB], writes=[metaVB], dma=True)
                else:
                    blk0 = tok0 // 128
                    for hh in range(NH):
                        op("pool", lambda e, hh=hh, blk0=blk0: e.dma_start(
                            out=v_d.ap()[hh][:, blk0 * 129:(blk0 + 4) * 129].rearrange("p (b d) -> p b d", b=4),
                            in_=vst[:, :, hh, :]),
                           reads=[vstB], writes=[vB[tl]], dma=True)

            load_x(0)
            if J > 1:
                load_x(1)
            load_wd(ctx, 0)
            op("sp", lambda e: e.dma_start(out=winS[:, 0:8192], in_=win_d.ap()[:, 0:8192]), reads=[wB["win"]], writes=[winSB], dma=True)
            op("sp", lambda e: e.dma_start(out=winS[:, 8192:16384], in_=win_d.ap()[:, 8192:16384]), reads=[wB["win"]], writes=[winSB], dma=True)
            N1(0)
            if J > 2:
                load_x(2)
            GU(0, 0, FC)
            if J > 1:
                N1(1)
            DN(0)
            def NMact1(ji):
                kind, tl, h, hB, nb, N = jinfo(ji)
                emit_norm_act1(ctx, h, hB, nb)

            def NMact2(ji):
                kind, tl, h, hB, nb, N = jinfo(ji)
                emit_norm_act2(ctx, h, hB, nb)

            def N1act1(ji):
                kind, tl, h, hB, nb, N = jinfo(ji)
                emit_norm_act1(ctx, h, hB, nb)

            def N1act2(ji):
                kind, tl, h, hB, nb, N = jinfo(ji)
                emit_norm_act2(ctx, h, hB, nb)

            def N1pe(ji):
                kind, tl, h, hB, nb, N = jinfo(ji)
                emit_norm_pe(ctx, nb, ctx.xnT[0], ctx.xnTB[0])

            for ji in range(J):
                nxt = ji + 1 < J
                hooks = {0: [lambda: NMact1(ji)], 2: [lambda: NMact2(ji)], 8: [lambda: NMpe(ji)]}
                if ji + 3 < J:
                    hooks[7] = [lambda: load_x(ji + 3)]
                if ji + 2 < J:
                    hooks[10] = [lambda: N1act1(ji + 2)]
                    hooks[12] = [lambda: N1act2(ji + 2)]
                if nxt:
                    run_gu(lambda j0, j1: GU(ji + 1, j0, j1), hooks)
                else:
                    for j in sorted(hooks):
                        for f in hooks[j]:
                            f()
                WI(ji)
                if ji + 2 < J:
                    N1pe(ji + 2)
                if nxt:
                    DN(ji + 1)
        S.barrier()

        with ExitStack() as ps:
            KW = NMETA + SMAX
            NBM = SMAX // 128
            kT = [sb(ps, "kT%d" % i, [128, KW], BF16) for i in range(2)]
            qT = [sb(ps, "qT%d" % i, [128, SMAX], BF16) for i in range(2)]
            Vs = [sb(ps, "V%d" % i, [128, NBM * 129], BF16) for i in range(2)]
            bia = [sb(ps, "bia%d" % i, [128, 6 * 512], F32) for i in range(2)]
            biam = [sb(ps, "biam%d" % i, [NMETA, 512], F32) for i in range(2)]
            kTB = [Buf("kT%d" % i) for i in range(2)]
            qTB = [Buf("qT%d" % i) for i in range(2)]
            VB = [Buf("V%d" % i) for i in range(2)]
            biaB = [Buf("bia%d" % i) for i in range(2)]
            vmt = sb(ps, "vmt", [NMETA, NH * 129], BF16); vmtB = Buf("vmt")
            op("sp", lambda e: e.dma_start(out=vmt[:], in_=vm_d.ap()), reads=[metaVB], writes=[vmtB], dma=True)
            PT = [sb(ps, "PT%d" % i, [128, 2, 512], BF16) for i in range(3)]
            PTB = [Buf("PT%d" % i) for i in range(3)]
            tmp2 = [sb(ps, "tmp%d" % i, [128, 2, 512], F32) for i in range(3)]
            tmp2B = [[Buf("tmp%d_%d" % (i, c)) for c in range(2)] for i in range(3)]
            accSf = sb(ps, "accS", [128, 8 * 130], F32)
            accS = accSf[:].rearrange("p (k n) -> p k n", n=130)
            accSBk = [Buf("accS%d" % i) for i in range(3)]
            rr8 = sb(ps, "rr8", [128, 8], F32); rr8B = Buf("rr8")
            ssq4 = sb(ps, "ssq4", [128, 4], F32); ssq4B = Buf("ssq4")
            ln4 = sb(ps, "ln4", [128, 4], F32); ln4B = Buf("ln4")
            rs = sb(ps, "rs", [128, 4], F32); rsB = Buf("rs")
            ob = sb(ps, "ob", [128, 4, 128], F32); obB = Buf("ob")
            ob2 = sb(ps, "ob2", [128, 4, 128], F32); ob2B = Buf("ob2")
            on = sb(ps, "on", [128, 4, 128], BF16); onB = Buf("on")
            pend_a, pend_b = [], []
            bg32 = sb(ps, "bg32", [128, 2 * DC * 512], F32); bg32B = Buf("bg32")
            bg16 = sb(ps, "bg16", [128, 4 * 2048], BF16); bg16B = Buf("bg16")
            bs32, bs32B, bs16, bs16B = bg32, bg32B, bg16, bg16B
            bg_tasks = []

            def add_bg(tasks, gap):
                bg_tasks.append(tasks[0])
                bg_tasks.extend([None] * gap)
                bg_tasks.extend(tasks[1:])
            for j0 in range(0, FC, 4):
                add_bg(gu_tasks(1, j0, bg32, bg32B, bg16, bg16B, True), 18)
            for j2 in range(FC // 2):
                add_bg(wd_tasks(1, j2, bs32, bs32B, bs16, bs16B, True), 6)
            for c2 in range(DC // 2):
                add_bg(wout_tasks(c2, bs32, bs32B, bs16, bs16B, True), 6)
            add_bg(poolw_tasks(bs32, bs32B, bs16, bs16B, True), 6)
            bg_tasks.reverse()
            bgc = [0]

            def bg_tick():
                bgc[0] += 1
                if bg_tasks and bgc[0] % 2 == 0:
                    f = bg_tasks.pop()
                    if f is not None:
                        f()
            aoS1 = sb(ps, "aoS", [128, 512], BF16)
            aoS = [aoS1, aoS1]
            aoSB1 = Buf("aoS")
            aoSB = [aoSB1, aoSB1]
            lt = sb(ps, "lt", [128, 4], F32); ltB = Buf("lt")
            ljunk = sb(ps, "ljunk", [128, 64], F32)
            for k in range(2):
                op("dve", lambda e, k=k: e.scalar_tensor_tensor(out=ljunk[:], in0=lam4[:, 2 * k, :], scalar=1.0,
                                                                in1=lam4[:, 2 * k + 1, :], op0=ALU.mult, op1=ALU.mult,
                                                                accum_out=lt[:, k:k + 1]),
                   reads=[lamB], writes=[ltB])
            op("act", lambda e: e.activation(out=lt[:, 2:4], in_=lt[:, 0:2], func=AF.Exp), reads=[ltB], writes=[ltB])
            op("dve", lambda e: e.tensor_tensor(out=neglam[:], in0=lt[:, 3:4], in1=lt[:, 2:3], op=ALU.subtract),
               reads=[ltB], writes=[neglamB])
            op("dve", lambda e: e.tensor_scalar(out=neglam[:], in0=neglam[:], scalar1=-LAMBDA_INIT, scalar2=None, op0=ALU.add),
               reads=[neglamB], writes=[neglamB])

            ACC = [4, 5, 6]
            TRB = 7
            trbank = banks[TRB][:].bitcast(BF16)

            def acc_ap(k):
                return banks[ACC[k // 3]][:, (k % 3) * 130:(k % 3) * 130 + 129]

            pairs = [(s, hh) for s in range(len(seqs)) for hh in range(NH)]

            def load_pair(pi):
                s, hh = pairs[pi]
                par = pi % 2
                Sl = seqs[s]
                op("sp", lambda e: e.dma_start(out=kT[par][:, 0:NMETA], in_=kTm_d.ap()[hh]),
                   reads=[metaKB], writes=[kTB[par]], dma=True)
                tls = [(s, t) for t in range(Sl // T)]
                op("sp", lambda e: e.dma_start(out=kT[par][:, NMETA:NMETA + Sl], in_=kT_d.ap()[hh][:, toff[s]:toff[s] + Sl]),
                   reads=[kB[tl] for tl in tls] + [kTB[par]], writes=[kTB[par]], dma=True)
                op("sp", lambda e: e.dma_start(out=qT[par][:, 0:Sl], in_=qT_d.ap()[hh][:, toff[s]:toff[s] + Sl]),
                   reads=[qB[tl] for tl in tls], writes=[qTB[par]], dma=True)
                nbs = nblk_s[s]
                op("sp", lambda e: e.dma_start(out=Vs[par][:, 0:nbs * 129],
                                               in_=v_d.ap()[hh][:, blk0_s[s] * 129:(blk0_s[s] + nbs) * 129]),
                   reads=[vB[tl] for tl in tls], writes=[VB[par]], dma=True)
                op("sp", lambda e: e.dma_start(out=bia[par][:], in_=bias_d.ap()[hh]), writes=[biaB[par]], dma=True)
                op("sp", lambda e: e.dma_start(out=biam[par][:], in_=biasm_d.ap()[hh]), reads=[biaB[par]], writes=[biaB[par]], dma=True)

            steps = []
            for pi, (s_, hh_) in enumerate(pairs):
                kbs_ = [-1] + list(range(nblk_s[s_]))
                for qt_ in range(seqs[s_] // T):
                    for idx_, kb_ in enumerate(kbs_):
                        steps.append((pi, s_, hh_, qt_, kb_, idx_, len(kbs_)))

            def emit_qk(g):
                pi, s, hh, qt, kb, idx, nk = steps[g]
                par = pi % 2
                q0 = qt * T
                p2 = g % 2
                np_ = NMETA if kb < 0 else 128
                k0 = 0 if kb < 0 else NMETA + kb * 128

                def mm(e):
                    ins = None
                    for c in range(2):
                        ins = e.matmul(banks[2 * p2 + c][0:np_, :], lhsT=kT[par][c * 64:(c + 1) * 64, k0:k0 + np_],
                                       rhs=qT[par][c * 64:(c + 1) * 64, q0:q0 + T], start=True, stop=True)
                    return ins
                op("pe", mm, reads=[kTB[par], qTB[par]], writes=[bankB[2 * p2], bankB[2 * p2 + 1]])

            def emit_exp(g):
                pi, s, hh, qt, kb, idx, nk = steps[g]
                par = pi % 2
                p2 = g % 2
                p3 = g % 3
                np_ = NMETA if kb < 0 else 128
                delta = kb - 4 * qt
                if kb < 0:
                    near = (qt == 0)
                    btile = biam[par][0:NMETA, :]
                    side = 0
                else:
                    near = (-1 <= delta <= 4)
                    btile = bia[par][:, (delta + 1) * 512:(delta + 2) * 512] if near else None
                    side = 0 if delta < 0 else 1
                sbk = [bankB[2 * p2], bankB[2 * p2 + 1]]
                tmp, tmpB = tmp2[p3], tmp2B[p3]
                if near:
                    op("dve", lambda e: e.tensor_tensor(
                        out=tmp[0:np_, :, :], in0=pall[0:np_, 2 * p2:2 * p2 + 2, :],
                        in1=btile.unsqueeze(1).broadcast_to([np_, 2, 512]), op=ALU.add),
                       reads=sbk + [biaB[par]], writes=tmpB)
                    op("act", lambda e: e.activation(out=PT[p3][0:np_, :, :], in_=tmp[0:np_, :, :], func=AF.Exp,
                                                     scale=0.125),
                       reads=tmpB, writes=[PTB[p3]])
                else:
                    op("act", lambda e: e.activation(
                        out=PT[p3][0:np_, :, :], in_=pall[0:np_, 2 * p2:2 * p2 + 2, :], func=AF.Exp,
                        bias=cb[0:np_, side, hh:hh + 1], scale=0.125),
                       reads=sbk + [cbB], writes=[PTB[p3]])

            def emit_av(g):
                pi, s, hh, qt, kb, idx, nk = steps[g]
                par = pi % 2
                p3 = g % 3
                first = (idx == 0)
                last = (idx == nk - 1)

                def mm(e):
                    ins = None
                    for c in range(2):
                        for i in range(4):
                            k = c * 4 + i
                            st_flag = first and (k % 3 == 0)
                            if kb < 0:
                                ins = e.matmul(acc_ap(k), lhsT=PT[p3][0:NMETA, c, i * 128:(i + 1) * 128],
                                               rhs=vmt[0:NMETA, hh * 129:(hh + 1) * 129],
                                               start=st_flag, stop=last, skip_group_check=True)
                            else:
                                ins = e.matmul(acc_ap(k), lhsT=PT[p3][:, c, i * 128:(i + 1) * 128],
                                               rhs=Vs[par][:, kb * 129:(kb + 1) * 129],
                                               start=st_flag, stop=last, skip_group_check=True)
                    return ins
                op("pe", mm, reads=[PTB[p3], VB[par], vmtB], writes=[bankB[b_] for b_ in ACC])

            load_pair(0)
            qtc = [0]
            emit_qk(0)
            emit_qk(1)
            for g in range(len(steps)):
                pi, s, hh, qt, kb, idx, nk = steps[g]
                q0 = qt * T
                if idx == 0 and qt == 0 and pi + 1 < len(pairs):
                    load_pair(pi + 1)
                emit_exp(g)
                if g + 2 < len(steps):
                    emit_qk(g + 2)
                emit_av(g)
                bg_tick()
                if idx == 5 and pend_a:
                    pend_a.pop()()
                if idx == 9 and pend_b:
                    pend_b.pop()()
                if idx == nk - 1:
                    if pend_a:
                        pend_a.pop()()
                    if pend_b:
                        pend_b.pop()()
                    for b_ in range(3):
                        ncol = 390 if b_ < 2 else 260
                        op("dve", lambda e, b_=b_, ncol=ncol: e.tensor_copy(
                            out=accSf[:, b_ * 390:b_ * 390 + ncol], in_=banks[ACC[b_]][:, 0:ncol]),
                           reads=[bankB[ACC[b_]]], writes=[accSBk[b_]])
                    a2 = qtc[0] % 2
                    qtc[0] += 1
                    tok0 = toff[s] + q0

                    def ep_a(hh=hh):
                        op("dve", lambda e: e.reciprocal(out=rr8[:, :], in_=accS[:, :, 128]), reads=accSBk, writes=[rr8B])
                        op("dve", lambda e: e.tensor_scalar(out=rr8[:, 4:8], in0=rr8[:, 4:8], scalar1=neglam[:, 0:1],
                                                            scalar2=None, op0=ALU.mult),
                           reads=[rr8B, neglamB], writes=[rr8B])
                        op("dve", lambda e: e.tensor_tensor(out=ob[:, :, :], in0=accS[:, 0:4, 0:128],
                                                            in1=rr8[:, 0:4].unsqueeze(2).broadcast_to([128, 4, 128]), op=ALU.mult),
                           reads=accSBk + [rr8B], writes=[obB])
                        op("dve", lambda e: e.tensor_tensor(out=ob2[:, :, :], in0=accS[:, 4:8, 0:128],
                                                            in1=rr8[:, 4:8].unsqueeze(2).broadcast_to([128, 4, 128]), op=ALU.mult),
                           reads=accSBk + [rr8B], writes=[ob2B])
                        op("dve", lambda e: e.tensor_tensor(out=ob[:, :, :], in0=ob[:, :, :], in1=ob2[:, :, :], op=ALU.add),
                           reads=[obB, ob2B], writes=[obB])
                        op("dve", lambda e: e.tensor_tensor(out=ob2[:, :, :], in0=ob[:, :, :], in1=ob[:, :, :], op=ALU.mult),
                           reads=[obB], writes=[ob2B])
                        op("dve", lambda e: e.reduce_sum(out=ssq4[:, :], in_=ob2[:, :, :], axis=mybir.AxisListType.X),
                           reads=[ob2B], writes=[ssq4B])
                        op("act", lambda e: e.activation(out=ln4[:, :], in_=ssq4[:, :], func=AF.Ln, bias=eps_t[:, 0:1], scale=1.0 / 128),
                           reads=[ssq4B, zB], writes=[ln4B])
                        op("act", lambda e: e.activation(out=rs[:, :], in_=ln4[:, :], func=AF.Exp, scale=-0.5),
                           reads=[ln4B], writes=[rsB])
                        op("dve", lambda e: e.tensor_tensor(out=on[:, :, :], in0=ob[:, :, :],
                                                            in1=rs[:, 0:4].unsqueeze(2).broadcast_to([128, 4, 128]), op=ALU.mult),
                           reads=[obB, rsB], writes=[onB])

                    def ep_b(hh=hh, a2=a2, tok0=tok0, s=s, qt=qt):
                        def trs(e):
                            ins = None
                            for i in range(4):
                                ins = e.transpose(trbank[:, i * 128:(i + 1) * 128], on[:, i, :], ident[:])
                            return ins
                        op("pe", trs, reads=[onB, identB], writes=[bankB[TRB]])
                        op("dve", lambda e: e.tensor_copy(out=aoS[a2][:], in_=trbank[:, 0:512]),
                           reads=[bankB[TRB]], writes=[aoSB[a2]])
                        op("pool", lambda e: e.dma_start(out=ao_d.ap()[hh][:, tok0:tok0 + T], in_=aoS[a2][:]),
                           reads=[aoSB[a2]], writes=[aoB[(s, qt)]], dma=True)
                    if pend_a:
                        pend_a.pop()()
                    if pend_b:
                        pend_b.pop()()
                    pend_a.append(ep_a)
                    pend_b.append(ep_b)
            if pend_a:
                pend_a.pop()()
            if pend_b:
                pend_b.pop()()
            while bg_tasks:
                f = bg_tasks.pop()
                if f is not None:
                    f()
        S.barrier()

        with ExitStack() as ps:
            ctx = common_ctx(ps, 1)
            woutS = sb(ps, "woutS", [128, DC * D], BF16); woutSB = Buf("woutS")
            op("sp", lambda e: e.dma_start(out=woutS[:], in_=wout_d.ap()), reads=[wB["wout"]], writes=[woutSB], dma=True)
            pwS = sb(ps, "pwS", [128, 512], BF16); pwSB = Buf("pwS")
            op("sp", lambda e: e.dma_start(out=pwS[:], in_=poolw_d.ap()), reads=[wB["poolw"]], writes=[pwSB], dma=True)
            mixT = sb(ps, "mixT", [128, DC, T], BF16)
            mixAB = Buf("mixA")
            mixPB = [Buf("mixP%d" % g) for g in range(4)]
            ph = sb(ps, "ph", [128, 4, T + 16], F32)
            phB = Buf("ph")
            pt1 = sb(ps, "pt1", [128, T + 16], F32); pt2 = sb(ps, "pt2", [128, T + 16], F32)
            pt1B, pt2B = Buf("pt1"), Buf("pt2")
            pooled = sb(ps, "pooled", [128, 4, T], BF16); pooledB = [Buf("pooled%d" % g) for g in range(4)]
            ws = WStream(ps, [(wgu_d[1], j, wB["wgu2"]) for _ in tiles for j in range(FC)])
            ws.prime()
            n_t = len(tiles)

            def tinfo(ti):
                s, t = tiles[ti]
                return s, t, toff[s] + t * T, ctx.hbuf[ti % 3], ctx.hB[ti % 3]

            def LDh(ti):
                s, t, tok0, h, hB = tinfo(ti)
                op("sp", lambda e: e.dma_start(out=h[:], in_=h1_d.ap()[tok0:tok0 + T, :].rearrange("(b p) d -> p b d", p=128)),
                   reads=[h1B[(s, t)]], writes=hB, dma=True)

            def LDm(ti):
                s, t, tok0, h, hB = tinfo(ti)
                op("sp", lambda e: e.dma_start(out=mixT[:, 0:4, :], in_=ao_d.ap()[:, :, tok0:tok0 + T].rearrange("h p n -> p h n")),
                   reads=[aoB[(s, t)]], writes=[mixAB], dma=True)
                c0 = pcol[s] + t * T
                rds = [pB[(s, t)], pEdgeB[s]]
                if t > 0:
                    rds.append(pB[(s, t - 1)])
                if (s, t + 1) in pB:
                    rds.append(pB[(s, t + 1)])
                op("sp", lambda e: e.dma_start(out=ph[:], in_=pT_d.ap()[:, :, c0:c0 + T + 16].rearrange("g p n -> p g n")),
                   reads=rds, writes=[phB], dma=True)

            def PLdve(ti, groups=(0, 1, 2, 3)):
                s, t, tok0, h, hB = tinfo(ti)
                P = ph
                last_tile = (t == seqs[s] // T - 1)
                for g, w in enumerate((2, 4, 8, 16)):
                    if g not in groups:
                        continue
                    a = 8 - w // 2
                    lens = T + w - 2
                    cur, curB = pt1, pt1B
                    nxt, nxtB = pt2, pt2B
                    op("dve", lambda e, g=g, a=a, lens=lens, cur=cur: e.tensor_tensor(
                        out=cur[:, 0:lens], in0=P[:, g, a:a + lens], in1=P[:, g, a + 1:a + 1 + lens], op=ALU.add),
                       reads=[phB], writes=[curB])
                    sh = 2
                    while sh < w:
                        lens -= sh
                        op("dve", lambda e, lens=lens, sh=sh, cur=cur, nxt=nxt: e.tensor_tensor(
                            out=nxt[:, 0:lens], in0=cur[:, 0:lens], in1=cur[:, sh:sh + lens], op=ALU.add),
                           reads=[curB], writes=[nxtB])
                        cur, curB, nxt, nxtB = nxt, nxtB, cur, curB
                        sh *= 2
                    op("dve", lambda e, g=g, w=w, cur=cur: e.scalar_tensor_tensor(
                        out=pooled[:, g, :], in0=cur[:, 0:T], scalar=1.0 / w, in1=P[:, g, 8:8 + T],
                        op0=ALU.mult, op1=ALU.subtract),
                       reads=[curB, phB], writes=[pooledB[g]])
                    if last_tile:
                        for tt in range(T - w // 2 + 1, T):
                            cntv = w - (tt + w // 2 - T)
                            op("dve", lambda e, g=g, tt=tt, cntv=cntv, cur=cur: e.scalar_tensor_tensor(
                                out=pooled[:, g, tt:tt + 1], in0=cur[:, tt:tt + 1], scalar=1.0 / cntv,
                                in1=P[:, g, 8 + tt:9 + tt], op0=ALU.mult, op1=ALU.subtract),
                               reads=[curB, phB], writes=[pooledB[g]])

            def WO(ti):
                s, t, tok0, h, hB = tinfo(ti)
                for g in range(4):
                    pq, pqB = mmp.get()
                    op("pe", lambda e, pq=pq, g=g: e.matmul(pq[:, :], lhsT=pwS[:, g * 128:(g + 1) * 128], rhs=pooled[:, g, :],
                                                           start=True, stop=True),
                       reads=[pwSB, pooledB[g]], writes=[pqB])
                    op("act", lambda e, pq=pq, g=g: e.activation(out=mixT[:, 4 + g, :], in_=pq[:, :], func=AF.Copy),
                       reads=[pqB], writes=[mixPB[g]])
                for b in range(4):
                    for half in range(2):
                        po, poB = mmp.get()

                        def mmo(e, po=po, b=b, half=half):
                            ins = None
                            for c in range(DC):
                                ins = e.matmul(po[:, :], lhsT=mixT[:, c, b * 128:(b + 1) * 128],
                                               rhs=woutS[:, c * D + half * 512:c * D + (half + 1) * 512],
                                               start=(c == 0), stop=(c == DC - 1))
                            return ins
                        op("pe", mmo, reads=[woutSB, mixAB] + mixPB, writes=[poB])
                        op("dve", lambda e, po=po, b=b, half=half: e.tensor_tensor(
                            out=h[:, b, half * 512:(half + 1) * 512], in0=po[:, :], in1=h[:, b, half * 512:(half + 1) * 512],
                            op=ALU.add),
                           reads=[poB, hB[b]], writes=[hB[b]])

            def N2act(ti):
                s, t, tok0, h, hB = tinfo(ti)
                emit_norm_act(ctx, h, hB, 4)

            def N2pe(ti):
                emit_norm_pe(ctx, 4, ctx.xnT[0], ctx.xnTB[0])

            def GU(ti, j0, j1):
                emit_gu(ctx, 4, T, ws, ctx.xnT[0], ctx.xnTB[0], j0, j1)

            def DN(ti):
                s, t, tok0, h, hB = tinfo(ti)
                emit_dn(ctx, h, hB, 4)

            def FINact(ti):
                s, t, tok0, h, hB = tinfo(ti)
                for b in range(4):
                    op("act", lambda e, b=b: e.activation(out=ctx.junk[:], in_=h[:, b, :], func=AF.Square,
                                                          accum_out=ctx.ssq[:, b:b + 1]),
                       reads=[hB[b]], writes=[ctx.junkB, ctx.ssqB[b]])
                emit_rstd(ctx, 4, D)

            def FINb(ti, b):
                s, t, tok0, h, hB = tinfo(ti)
                op("dve", lambda e: e.scalar_tensor_tensor(out=h[:, b, :], in0=h[:, b, :], scalar=ctx.rstd[:, b:b + 1],
                                                           in1=gfin_bc[:], op0=ALU.mult, op1=ALU.mult),
                   reads=[hB[b], ctx.rstdB[b], gfinB], writes=[hB[b]])

            def FINst(ti):
                s, t, tok0, h, hB = tinfo(ti)
                op("pool", lambda e: e.dma_start(
                    out=y.ap()[tok0:tok0 + T, :].rearrange("(b p) d -> p b d", p=128), in_=h[:]),
                   reads=hB, writes=[Buf("ydump")], dma=True)

            LDh(0); LDm(0)
            load_wd(ctx, 1)
            if n_t > 1:
                LDh(1)
            PLdve(0); WO(0)
            if n_t > 1:
                LDm(1)
            N2act(0); N2pe(0)
            if n_t > 2:
                LDh(2)
            def FIN(ti):
                FINact(ti)
                for b in range(4):
                    FINb(ti, b)
                FINst(ti)

            for ti in range(n_t):
                nxt = ti + 1 < n_t
                hooks = {}
                if ti > 0:
                    hooks[1] = [lambda: FINact(ti - 1)]
                    for b in range(4):
                        hooks.setdefault(3 + b, []).append(lambda b=b: FINb(ti - 1, b))
                    hooks[6].append(lambda: FINst(ti - 1))
                    if ti + 2 < n_t:
                        hooks.setdefault(15, []).append(lambda: LDh(ti + 2))
                if nxt:
                    for g in range(4):
                        hooks.setdefault(8 + 2 * g, []).append(lambda g=g: PLdve(ti + 1, (g,)))
                run_gu(lambda j0, j1: GU(ti, j0, j1), hooks)
                if nxt:
                    WO(ti + 1)
                    if ti + 2 < n_t:
                        LDm(ti + 2)
                    N2act(ti + 1)
                DN(ti)
                if nxt:
                    N2pe(ti + 1)
            FIN(n_t - 1)
        S.final_wait("sp")
        S.final_wait("pool")
    return nc


_CACHE = {}


def _core_inputs(inputs, xs):
    c = host_consts()
    f = lambda a: np.ascontiguousarray(np.asarray(a, dtype=np.float32))
    m = {
        "x": np.ascontiguousarray(xs), "meta": f(inputs["meta_tokens"]), "table": f(inputs["rel_bias_table"]),
        "g1": f(inputs["norm_ffn1"][0]), "gmix": f(inputs["norm_mix"][0]), "g2": f(inputs["norm_ffn2"][0]),
        "gfin": f(inputs["norm_final"]),
        "wg1": f(inputs["ffn1_w_gate"][0]), "wu1": f(inputs["ffn1_w_up"][0]), "wd1": f(inputs["ffn1_w_down"][0]),
        "wg2": f(inputs["ffn2_w_gate"][0]), "wu2": f(inputs["ffn2_w_up"][0]), "wd2": f(inputs["ffn2_w_down"][0]),
        "win": f(inputs["w_in"][0]), "wout": f(inputs["w_out"][0]),
        "lq1": f(inputs["lambda_q1"][0]), "lk1": f(inputs["lambda_k1"][0]),
        "lq2": f(inputs["lambda_q2"][0]), "lk2": f(inputs["lambda_k2"][0]),
        "subln": f(inputs["subln_gain"][0]), "poolw": f(inputs["pool_w"][0]), "pscale": f(inputs["pool_scale"][0]),
        "erevT": c["erevT"], "ident": c["ident"], "jflip": c["jflip"],
    }
    return m


def kernel(**inputs):
    xp = np.asarray(inputs["x_prompt"], dtype=np.float32)
    xsm = np.asarray(inputs["x_sample"], dtype=np.float32)
    n = 8
    Bp, Sp, _ = xp.shape
    Bs, Ss, _ = xsm.shape
    ppc = Bp // n
    spc = Bs // n
    seqs = [Sp] * ppc + [Ss] * spc
    key = tuple(seqs)
    if key not in _CACHE:
        _CACHE[key] = build(seqs)
    nc = _CACHE[key]
    in_maps = []
    for c in range(n):
        parts = [xp[c * ppc + i] for i in range(ppc)] + [xsm[c * spc + i] for i in range(spc)]
        in_maps.append(_core_inputs(inputs, np.concatenate(parts, axis=0)))
    res = run_bass_kernel_spmd(nc, in_maps, core_ids=list(range(n)))
    yp = np.empty_like(xp)
    ys = np.empty_like(xsm)
    for c in range(n):
        yc = res.results[c]["y"]
        o = 0
        for i in range(ppc):
            yp[c * ppc + i] = yc[o:o + Sp]
            o += Sp
        for i in range(spc):
            ys[c * spc + i] = yc[o:o + Ss]
            o += Ss
    return (yp, ys)
```

```python
import math
from contextlib import ExitStack

import numpy as np
import ml_dtypes

import concourse.bass as bass
import concourse.mybir as mybir
from concourse.bass_utils import run_bass_kernel_spmd

F32 = mybir.dt.float32
BF16 = mybir.dt.bfloat16
AF = mybir.ActivationFunctionType
ALU = mybir.AluOpType

D = 1024
DC = 8
FF = 2816
FC = 22
T = 512
NMETA = 16
EPS = 1e-6
NH = 4
N0 = 640
NR = 1280
LAMBDA_INIT = 0.8 - 0.6 * math.exp(-0.3 * 0)
NS_RING = 4


class Buf:
    __slots__ = ("name", "w", "r")

    def __init__(self, name):
        self.name = name
        self.w = {}
        self.r = {}


class Sched:
    def __init__(self, nc, stack, n_dma=28):
        self.nc = nc
        self.eng = {"pe": nc.tensor, "act": nc.scalar, "dve": nc.vector, "pool": nc.gpsimd, "sp": nc.sync}
        self.semh = {}
        self.cnt = {}
        for k in self.eng:
            self.semh[k] = stack.enter_context(nc.semaphore("e_" + k))
            self.cnt[k] = 0
        self.nd = {"sp": n_dma - 12, "pool": 12}
        for q, n in self.nd.items():
            for i in range(n):
                self.semh[("d", q, i)] = stack.enter_context(nc.semaphore("d_%s_%d" % (q, i)))
                self.cnt[("d", q, i)] = 0
        self.dnext = {"sp": 0, "pool": 0}
        self.clock = {k: {} for k in self.eng}
        self.evclock = {}
        self.nops = 0

    def _merge(self, dst, src):
        for k, v in src.items():
            if dst.get(k, 0) < v:
                dst[k] = v

    def _wait(self, eng, need, skip_self_pe=True):
        e = self.eng[eng]
        ck = self.clock[eng]
        for k, v in need.items():
            if skip_self_pe and k == "pe" and eng == "pe":
                continue
            if ck.get(k, 0) >= v:
                continue
            e.wait_ge(self.semh[k], v)
            if ck.get(k, 0) < v:
                ck[k] = v
            ec = self.evclock.get((k, v))
            if ec is not None:
                self._merge(ck, ec)

    def op(self, eng, emit, reads=(), writes=(), dma=False):
        need = {}

        def add(d):
            for k, v in d.items():
                if need.get(k, 0) < v:
                    need[k] = v

        for b in reads:
            add(b.w)
        selfkey = "pe" if (eng == "pe" and not dma) else None
        for b in writes:
            for d in (b.w, b.r):
                for k, v in d.items():
                    if k != selfkey and need.get(k, 0) < v:
                        need[k] = v
        if dma:
            i = self.dnext[eng]
            self.dnext[eng] = (i + 1) % self.nd[eng]
            key = ("d", eng, i)
            if self.cnt[key] > 0 and need.get(key, 0) < self.cnt[key]:
                need[key] = self.cnt[key]
            inc = 16
        else:
            key = eng
            inc = 1
        self._wait(eng, need)
        self.cnt[key] += inc
        val = self.cnt[key]
        ins = emit(self.eng[eng])
        ins.then_inc(self.semh[key], inc)
        ec = dict(self.clock[eng])
        ec[key] = val
        if eng == "pe" and not dma:
            self.clock["pe"]["pe"] = val - 1
            ec["pe"] = val
        self.evclock[(key, val)] = ec
        for b in writes:
            b.w = {key: val}
            b.r = {}
        for b in reads:
            if b.r.get(key, 0) < val:
                b.r[key] = val
        self.nops += 1
        return {key: val}

    def barrier(self):
        allev = {k: v for k, v in self.cnt.items() if v > 0}
        for eng in self.eng:
            self._wait(eng, allev, skip_self_pe=False)

    def final_wait(self, eng="sp"):
        allev = {k: v for k, v in self.cnt.items() if v > 0}
        self._wait(eng, allev, skip_self_pe=False)


def _rel_bucket_np(rel):
    half = 16
    max_exact = 8
    ret = np.where(rel > 0, half, 0)
    n = np.abs(rel)
    nf = np.maximum(n, 1).astype(np.float32)
    large = max_exact + (np.log(nf / np.float32(max_exact)) / np.float32(math.log(128 / max_exact))
                         * np.float32(half - max_exact)).astype(np.int32)
    large = np.minimum(large, half - 1)
    return ret + np.where(n < max_exact, n, large)


def host_consts():
    rel = N0 - np.arange(NR, dtype=np.int64)
    bk = _rel_bucket_np(rel)
    erev = np.zeros((32, NR), np.float32)
    erev[bk, np.arange(NR)] = 1.0
    ident = np.eye(128, dtype=np.float32).astype(ml_dtypes.bfloat16)
    jflip = np.ascontiguousarray(np.eye(128, dtype=np.float32)[::-1])
    return {"erevT": erev, "ident": ident, "jflip": jflip}


def build(seqs):
    nc = bass.Bass("TRN2", target_bir_lowering=False)
    NTOK = sum(seqs)
    NBLK = NTOK // 128
    toff = [sum(seqs[:i]) for i in range(len(seqs))]
    pcol = [sum(s + 16 for s in seqs[:i]) for i in range(len(seqs))]
    PW = sum(s + 16 for s in seqs)
    SMAX = max(seqs)

    def din(name, shape, dt=F32):
        return nc.dram_tensor(name, list(shape), dt, kind="ExternalInput")

    x = din("x", [NTOK, D])
    meta = din("meta", [NMETA, D])
    table = din("table", [32, NH])
    g1 = din("g1", [D]); gmix = din("gmix", [D]); g2 = din("g2", [D]); gfin = din("gfin", [D])
    wg1 = din("wg1", [D, FF]); wu1 = din("wu1", [D, FF]); wd1 = din("wd1", [FF, D])
    wg2 = din("wg2", [D, FF]); wu2 = din("wu2", [D, FF]); wd2 = din("wd2", [FF, D])
    win = din("win", [D, 2048]); wout = din("wout", [D, D])
    lq1 = din("lq1", [64]); lk1 = din("lk1", [64]); lq2 = din("lq2", [64]); lk2 = din("lk2", [64])
    subln = din("subln", [128]); poolw = din("poolw", [4, 128, 128]); pscale = din("pscale", [512])
    erevT = din("erevT", [32, NR]); ident_d = din("ident", [128, 128], BF16); jflip_d = din("jflip", [128, 128])
    y = nc.dram_tensor("y", [NTOK, D], F32, kind="ExternalOutput")

    def dscr(name, shape, dt):
        return nc.dram_tensor(name, list(shape), dt)

    wgu_d = [dscr("wgu%d" % i, [FC, 128, 2048], BF16) for i in (1, 2)]
    wd_d = [dscr("wdb%d" % i, [128, FC * D], BF16) for i in (1, 2)]
    win_d = dscr("winb", [128, DC * 2048], BF16)
    wout_d = dscr("woutb", [128, DC * D], BF16)
    poolw_d = dscr("poolwb", [128, 4 * 128], BF16)
    h1_d = dscr("h1d", [NTOK, D], F32)
    qT_d = dscr("qTd", [NH, 128, NTOK], BF16)
    kT_d = dscr("kTd", [NH, 128, NTOK], BF16)
    ao_d = dscr("aoTd", [NH, 128, NTOK], BF16)
    v_d = dscr("vd", [NH, 128, NBLK * 129], BF16)
    pT_d = dscr("pTd", [4, 128, PW], F32)
    kTm_d = dscr("kTm", [NH, 128, NMETA], BF16)
    vm_d = dscr("vmd", [NMETA, NH * 129], BF16)
    tr_d = dscr("trd", [NH, NR], F32)
    bias_d = dscr("biasd", [NH, 128, 6 * 512], F32)
    biasm_d = dscr("biasmd", [NH, NMETA, 512], F32)

    with ExitStack() as gs:
        S = Sched(nc, gs)
        op = S.op

        uid = [0]

        def sb(stack, name, shape, dt):
            uid[0] += 1
            return stack.enter_context(nc.sbuf_tensor("sb%d_%s" % (uid[0], name), list(shape), dt))

        pall = gs.enter_context(nc.psum_tensor("pall", [128, 8, 512], F32))
        banks = [pall[:, i, :] for i in range(8)]
        bankB = [Buf("bank%d" % i) for i in range(8)]

        ident = sb(gs, "ident", [128, 128], BF16)
        identB = Buf("ident")
        op("sp", lambda e: e.dma_start(out=ident[:], in_=ident_d.ap()), writes=[identB], dma=True)
        gT = sb(gs, "gT", [128, 3, DC], F32)
        gTB = Buf("gT")
        for k, g in enumerate((g1, gmix, g2)):
            op("sp", lambda e, k=k, g=g: e.dma_start(out=gT[:, k, :], in_=g.ap().rearrange("(c p) -> p c", p=128),
                                                    allow_slow_non_contiguous=True),
               writes=[gTB], dma=True)
        scT = sb(gs, "scT", [128, DC], F32)
        scTB = Buf("scT")
        sublT = sb(gs, "sublT", [128, 1], F32)
        sublB = Buf("sublT")
        op("sp", lambda e: e.dma_start(out=sublT[:], in_=subln.ap().rearrange("(p o) -> p o", o=1)),
           writes=[sublB], dma=True)
        op("sp", lambda e: e.dma_start(out=scT[:, 4:8], in_=pscale.ap().rearrange("(g p) -> p g", p=128),
                                       allow_slow_non_contiguous=True), writes=[scTB], dma=True)
        for c in range(4):
            op("dve", lambda e, c=c: e.tensor_scalar(out=scT[:, c:c + 1], in0=sublT[:], scalar1=1.0 - LAMBDA_INIT,
                                                     scalar2=None, op0=ALU.mult),
               reads=[sublB], writes=[scTB])
        gfin_bc = sb(gs, "gfin_bc", [128, D], F32)
        gfinB = Buf("gfin")
        op("sp", lambda e: e.dma_start(out=gfin_bc[:], in_=gfin.ap().rearrange("(o d) -> o d", o=1).broadcast_to([128, D])),
           writes=[gfinB], dma=True)
        lam4 = sb(gs, "lam4", [128, 4, 64], F32)
        lamB = Buf("lam4")
        for k, l in enumerate((lq1, lk1, lq2, lk2)):
            op("sp", lambda e, k=k, l=l: e.dma_start(out=lam4[:, k, :],
                                                    in_=l.ap().rearrange("(o d) -> o d", o=1).broadcast_to([128, 64])),
               writes=[lamB], dma=True)
        cb = sb(gs, "cb", [128, 2, NH], F32)
        cbB = Buf("cb")
        for k, row in enumerate((15, 31)):
            op("sp", lambda e, k=k, row=row: e.dma_start(out=cb[:, k, :], in_=table.ap()[row:row + 1, :].broadcast_to([128, NH])),
               writes=[cbB], dma=True)
        neglam = sb(gs, "neglam", [128, 1], F32)
        neglamB = Buf("neglam")
        ones_col = sb(gs, "ones_col", [128, 1], F32)
        zeros8 = sb(gs, "zeros8", [128, 4, 8], F32)
        zB = Buf("zeros")
        eps_t = sb(gs, "eps_t", [128, 1], F32)
        op("dve", lambda e: e.memset(eps_t[:], EPS), writes=[zB])
        op("dve", lambda e: e.memset(ones_col[:], 1.0), writes=[zB])
        op("dve", lambda e: e.memset(zeros8[:], 0.0), writes=[zB])

        wB = {k: Buf(k) for k in ("wgu1", "wgu2", "wd1", "wd2", "win", "wout", "poolw")}
        alt = [0]

        def conv(out_ap, in_ap, scale_ap, rB, wBf, dve_only=False):
            alt[0] ^= 1
            if alt[0] or dve_only:
                if scale_ap is None:
                    op("dve", lambda e: e.tensor_copy(out=out_ap, in_=in_ap), reads=rB, writes=wBf)
                else:
                    op("dve", lambda e: e.tensor_scalar(out=out_ap, in0=in_ap, scalar1=scale_ap, scalar2=None,
                                                        op0=ALU.mult), reads=rB, writes=wBf)
            else:
                if scale_ap is None:
                    op("act", lambda e: e.activation(out=out_ap, in_=in_ap, func=AF.Copy), reads=rB, writes=wBf)
                else:
                    op("act", lambda e: e.activation(out=out_ap, in_=in_ap, func=AF.Copy, scale=scale_ap),
                       reads=rB, writes=wBf)

        def gu_tasks(li, j0, s32, s32B, s16, s16B, dve_only):
            wg, wu, gk = ((wg1, wu1, 0), (wg2, wu2, 2))[li]
            nj = min(4, FC - j0)
            v32 = s32[:].rearrange("p (w c n) -> p w c n", w=2, c=DC)
            v16 = s16[:].rearrange("p (j w c n) -> p j w c n", j=4, w=2, c=DC)

            def ld():
                for w_i, wsrc in enumerate((wg, wu)):
                    op("sp", lambda e, wsrc=wsrc, w_i=w_i: e.dma_start(
                        out=v32[:, w_i, :, 0:nj * 128],
                        in_=wsrc.ap()[:, j0 * 128:(j0 + nj) * 128].rearrange("(c p) n -> p c n", p=128)),
                       writes=[s32B], dma=True)
            tasks = [ld]
            for w_i in range(2):
                for c0 in range(0, DC, 2):
                    def cv(w_i=w_i, c0=c0):
                        for c in (c0, c0 + 1):
                            conv(v16[:, 0:nj, w_i, c, :], v32[:, w_i, c, 0:nj * 128].rearrange("p (j n) -> p j n", j=nj),
                                 gT[:, gk, c:c + 1], [s32B, gTB], [s16B], dve_only)
                    tasks.append(cv)

            def st_():
                op("pool", lambda e: e.dma_start(
                    out=wgu_d[li].ap()[j0:j0 + nj].rearrange("j p n -> p j n"),
                    in_=s16[:, 0:nj * 2048].rearrange("p (j n) -> p j n", j=nj)),
                   reads=[s16B], writes=[wB["wgu%d" % (li + 1)]], dma=True)
            tasks.append(st_)
            return tasks

        def wd_tasks(li, j2, s32, s32B, s16, s16B, dve_only):
            wd = (wd1, wd2)[li]

            def ld():
                op("sp", lambda e: e.dma_start(
                    out=s32[:, 0:2048].rearrange("p (j n) -> p j n", j=2),
                    in_=wd.ap()[j2 * 256:(j2 + 1) * 256, :].rearrange("(j p) n -> p j n", p=128)),
                   writes=[s32B], dma=True)

            def cv():
                conv(s16[:, 0:1024], s32[:, 0:1024], None, [s32B], [s16B], dve_only)
                conv(s16[:, 1024:2048], s32[:, 1024:2048], None, [s32B], [s16B], dve_only)

            def st_():
                op("pool", lambda e: e.dma_start(out=wd_d[li].ap()[:, j2 * 2048:(j2 + 1) * 2048], in_=s16[:, 0:2048]),
                   reads=[s16B], writes=[wB["wd%d" % (li + 1)]], dma=True)
            return [ld, cv, st_]

        def win_tasks(c, s32, s32B, s16, s16B, dve_only):
            def ld():
                op("sp", lambda e: e.dma_start(out=s32[:, 0:2048], in_=win.ap()[c * 128:(c + 1) * 128, :]),
                   writes=[s32B], dma=True)

            def cv():
                conv(s16[:, 0:1024], s32[:, 0:1024], gT[:, 1, c:c + 1], [s32B, gTB], [s16B], dve_only)
                conv(s16[:, 1024:2048], s32[:, 1024:2048], gT[:, 1, c:c + 1], [s32B, gTB], [s16B], dve_only)

            def st_():
                op("pool", lambda e: e.dma_start(out=win_d.ap()[:, c * 2048:(c + 1) * 2048], in_=s16[:, 0:2048]),
                   reads=[s16B], writes=[wB["win"]], dma=True)
            return [ld, cv, st_]

        def wout_tasks(c2, s32, s32B, s16, s16B, dve_only):
            def ld():
                op("sp", lambda e: e.dma_start(
                    out=s32[:, 0:2048].rearrange("p (c n) -> p c n", c=2),
                    in_=wout.ap()[c2 * 256:(c2 + 1) * 256, :].rearrange("(c p) n -> p c n", p=128)),
                   writes=[s32B], dma=True)

            def cv():
                for cc in range(2):
                    c = c2 * 2 + cc
                    conv(s16[:, cc * 1024:(cc + 1) * 1024], s32[:, cc * 1024:(cc + 1) * 1024], scT[:, c:c + 1],
                         [s32B, scTB], [s16B], dve_only)

            def st_():
                op("pool", lambda e: e.dma_start(out=wout_d.ap()[:, c2 * 2048:(c2 + 1) * 2048], in_=s16[:, 0:2048]),
                   reads=[s16B], writes=[wB["wout"]], dma=True)
            return [ld, cv, st_]

        def poolw_tasks(s32, s32B, s16, s16B, dve_only):
            def ld():
                op("sp", lambda e: e.dma_start(out=s32[:, 0:512].rearrange("p (g d) -> p g d", g=4),
                                               in_=poolw.ap().rearrange("g c d -> c g d")),
                   writes=[s32B], dma=True)

            def cv():
                conv(s16[:, 0:512], s32[:, 0:512], None, [s32B], [s16B], dve_only)

            def st_():
                op("pool", lambda e: e.dma_start(out=poolw_d.ap(), in_=s16[:, 0:512]),
                   reads=[s16B], writes=[wB["poolw"]], dma=True)
            return [ld, cv, st_]

        with ExitStack() as ps:
            st32 = [sb(ps, "st32_%d" % i, [128, 2048], F32) for i in range(3)]
            st32B = [Buf("st32_%d" % i) for i in range(3)]
            st16 = [sb(ps, "st16_%d" % i, [128, 2048], BF16) for i in range(3)]
            st16B = [Buf("st16_%d" % i) for i in range(3)]
            g32 = [sb(ps, "g32_%d" % i, [128, 2 * DC * 512], F32) for i in range(3)]
            g32B = [Buf("g32_%d" % i) for i in range(3)]
            g16 = [sb(ps, "g16_%d" % i, [128, 4 * 2048], BF16) for i in range(3)]
            g16B = [Buf("g16_%d" % i) for i in range(3)]
            gi = 0
            for j0 in range(0, FC, 4):
                i = gi % 3
                gi += 1
                for f in gu_tasks(0, j0, g32[i], g32B[i], g16[i], g16B[i], False):
                    f()
            cnt = 0
            for j2 in range(FC // 2):
                i = cnt % 3
                cnt += 1
                for f in wd_tasks(0, j2, st32[i], st32B[i], st16[i], st16B[i], False):
                    f()
            for c in range(DC):
                i = cnt % 3
                cnt += 1
                for f in win_tasks(c, st32[i], st32B[i], st16[i], st16B[i], False):
                    f()

            tab_sb = sb(ps, "tab_sb", [32, NH], F32)
            er_sb = sb(ps, "er_sb", [32, NR], F32)
            tr_sb = sb(ps, "tr_sb", [NH, NR], F32)
            tabB, erB, trB, trdB = Buf("tab"), Buf("er"), Buf("tr"), Buf("trd")
            biasdB = Buf("biasd")
            op("sp", lambda e: e.dma_start(out=tab_sb[:], in_=table.ap()), writes=[tabB], dma=True)
            op("sp", lambda e: e.dma_start(out=er_sb[:], in_=erevT.ap()), writes=[erB], dma=True)
            for k3 in range(3):
                n0 = k3 * 512
                n1 = min(NR, n0 + 512)
                op("pe", lambda e, n0=n0, n1=n1, k3=k3: e.matmul(banks[k3][0:NH, 0:n1 - n0], lhsT=tab_sb[:, :],
                                                                 rhs=er_sb[:, n0:n1], start=True, stop=True),
                   reads=[tabB, erB], writes=[bankB[k3]])
                op("dve", lambda e, n0=n0, n1=n1, k3=k3: e.tensor_copy(out=tr_sb[:, n0:n1], in_=banks[k3][0:NH, 0:n1 - n0]),
                   reads=[bankB[k3]], writes=[trB])
            op("pool", lambda e: e.dma_start(out=tr_d.ap(), in_=tr_sb[:]), reads=[trB], writes=[trdB], dma=True)
            jf = sb(ps, "jf", [128, 128], F32); jfB = Buf("jf")
            op("sp", lambda e: e.dma_start(out=jf[:], in_=jflip_d.ap()), writes=[jfB], dma=True)
            hk = [sb(ps, "hk%d" % i, [128, 512], F32) for i in range(2)]; hkB = [Buf("hk%d" % i) for i in range(2)]
            bo = [sb(ps, "bo%d" % i, [128, 512], F32) for i in range(2)]; boB = [Buf("bo%d" % i) for i in range(2)]
            bi = 0
            for hh in range(NH):
                for dd in range(7):
                    i2 = bi % 2
                    bi += 1
                    if dd < 6:
                        npart = 128
                        base = N0 - 128 * (dd - 1) - 127
                        lhs = jf[:, :]
                    else:
                        npart = NMETA
                        base = N0 + 1
                        lhs = jf[0:NMETA, 112:128]
                    src = bass.AP(tr_d, hh * NR + base, [[1, npart], [1, 512]])
                    op("sp", lambda e: e.dma_start(out=hk[i2][0:npart, :], in_=src), reads=[trdB], writes=[hkB[i2]], dma=True)
                    bk = 4 + i2
                    op("pe", lambda e: e.matmul(banks[bk][0:npart, :], lhsT=lhs, rhs=hk[i2][0:npart, :], start=True, stop=True),
                       reads=[jfB, hkB[i2]], writes=[bankB[bk]])
                    op("dve", lambda e: e.tensor_scalar(out=bo[i2][0:npart, :], in0=banks[bk][0:npart, :], scalar1=8.0,
                                                        scalar2=None, op0=ALU.mult),
                       reads=[bankB[bk]], writes=[boB[i2]])
                    if dd < 6:
                        op("pool", lambda e: e.dma_start(out=bias_d.ap()[hh][:, dd * 512:(dd + 1) * 512], in_=bo[i2][:, :]),
                           reads=[boB[i2]], writes=[biasdB], dma=True)
                    else:
                        op("pool", lambda e: e.dma_start(out=biasm_d.ap()[hh], in_=bo[i2][0:NMETA, :]),
                           reads=[boB[i2]], writes=[biasdB], dma=True)
        S.barrier()

        def emit_rstd(ctx, nb, n_feat):
            op("act", lambda e: e.activation(out=ctx.lnv[:, 0:nb], in_=ctx.ssq[:, 0:nb], func=AF.Ln,
                                             bias=eps_t[:, 0:1], scale=1.0 / n_feat),
               reads=ctx.ssqB[0:nb] + [zB], writes=[ctx.lnvB])
            op("act", lambda e: e.activation(out=ctx.rstd[:, 0:nb], in_=ctx.lnv[:, 0:nb], func=AF.Exp, scale=-0.5),
               reads=[ctx.lnvB], writes=ctx.rstdB[0:nb])

        def emit_norm_act1(ctx, h, hB, nb):
            for b in range(nb):
                op("act", lambda e, b=b: e.activation(out=ctx.junk[:], in_=h[:, b, :], func=AF.Square,
                                                      accum_out=ctx.ssq[:, b:b + 1]),
                   reads=[hB[b]], writes=[ctx.junkB, ctx.ssqB[b]])
            emit_rstd(ctx, nb, D)

        def emit_norm_act(ctx, h, hB, nb):
            emit_norm_act1(ctx, h, hB, nb)
            emit_norm_act2(ctx, h, hB, nb)

        def emit_norm_act2(ctx, h, hB, nb):
            for b in range(nb):
                op("act", lambda e, b=b: e.activation(out=ctx.xn[b][:], in_=h[:, b, :], func=AF.Copy,
                                                      scale=ctx.rstd[:, b:b + 1]),
                   reads=[hB[b], ctx.rstdB[b]], writes=[ctx.xnB[b]])

        def emit_norm_pe(ctx, nb, xnT, xnTB):
            for b in range(nb):
                tb = 6 + (b % 2)
                tbank = banks[tb][:].bitcast(BF16)

                def tr(e, b=b, tbank=tbank):
                    ins = None
                    for c in range(DC):
                        ins = e.transpose(tbank[:, c * 128:(c + 1) * 128], ctx.xn[b][:, c * 128:(c + 1) * 128], ident[:])
                    return ins
                op("pe", tr, reads=[ctx.xnB[b], identB], writes=[bankB[tb]])
                op("dve", lambda e, b=b, tbank=tbank: e.tensor_copy(
                    out=xnT[:, :, b * 128:(b + 1) * 128], in_=tbank.rearrange("p (c n) -> p c n", c=DC)),
                   reads=[bankB[tb]], writes=[xnTB[b]])

        class MMPool:
            def __init__(self):
                self.i = 0

            def get(self):
                i = self.i
                self.i = (i + 1) % 6
                return banks[i], bankB[i]
        mmp = MMPool()

        class WStream:
            def __init__(self, stack, chunks, ns=NS_RING):
                self.chunks = chunks
                self.ns = ns
                self.slots = [sb(stack, "wslot%d" % i, [128, 2048], BF16) for i in range(ns)]
                self.slotB = [Buf("wslot%d" % i) for i in range(ns)]
                self.loaded = 0
                self.used = 0

            def _load(self):
                if self.loaded >= len(self.chunks):
                    return
                t, j, srcB = self.chunks[self.loaded]
                i = self.loaded % self.ns
                op("sp", lambda e: e.dma_start(out=self.slots[i][:], in_=t.ap()[j]),
                   reads=[srcB], writes=[self.slotB[i]], dma=True)
                self.loaded += 1

            def prime(self):
                for _ in range(self.ns - 1):
                    self._load()

            def get(self):
                self._load()
                i = self.used % self.ns
                self.used += 1
                return self.slots[i], self.slotB[i]

        def load_wd(ctx, li):
            jr = [(0, 6), (6, 12), (12, 17), (17, 22)]
            for k, (j0, j1) in enumerate(jr):
                op("sp", lambda e, j0=j0, j1=j1: e.dma_start(out=ctx.wd[:, j0 * D:j1 * D], in_=wd_d[li].ap()[:, j0 * D:j1 * D]),
                   reads=[wB["wd%d" % (li + 1)]], writes=[ctx.wdB[k]], dma=True)

        def emit_gu(ctx, nb, N, ws, xnT, xnTB, j0, j1):
            xr = [xnTB[b] for b in range(nb)]
            for j in range(j0, j1):
                slot, slotB = ws.get()
                pg, pgB = mmp.get()
                pu, puB = mmp.get()

                def mm2(e, pg=pg, pu=pu, slot=slot):
                    ins = None
                    for w_i, dst in ((0, pg), (1, pu)):
                        for c in range(DC):
                            o = (w_i * DC + c) * 128
                            ins = e.matmul(dst[:, 0:N], lhsT=slot[:, o:o + 128], rhs=xnT[:, c, 0:N],
                                           start=(c == 0), stop=(c == DC - 1))
                    return ins
                op("pe", mm2, reads=[slotB] + xr, writes=[pgB, puB])
                si = j % 2
                op("act", lambda e, pg=pg, si=si: e.activation(out=ctx.sg[si][:, 0:N], in_=pg[:, 0:N], func=AF.Silu),
                   reads=[pgB], writes=[ctx.sgB[si]])
                op("dve", lambda e, pu=pu, si=si, j=j: e.tensor_tensor(out=ctx.aT[:, j, 0:N], in0=pu[:, 0:N],
                                                                         in1=ctx.sg[si][:, 0:N], op=ALU.mult),
                   reads=[puB, ctx.sgB[si]], writes=[ctx.aTB[j]])

        def emit_dn(ctx, h, hB, nb):
            for b in range(nb):
                for half in range(2):
                    po, poB = mmp.get()

                    def mmd(e, po=po, b=b, half=half):
                        ins = None
                        for j in range(FC):
                            ins = e.matmul(po[:, :], lhsT=ctx.aT[:, j, b * 128:(b + 1) * 128],
                                           rhs=ctx.wd[:, j * D + half * 512:j * D + (half + 1) * 512],
                                           start=(j == 0), stop=(j == FC - 1))
                        return ins
                    op("pe", mmd, reads=ctx.aTB + ctx.wdB, writes=[poB])
                    op("dve", lambda e, po=po, b=b, half=half: e.scalar_tensor_tensor(
                        out=h[:, b, half * 512:(half + 1) * 512], in0=po[:, :], scalar=0.5,
                        in1=h[:, b, half * 512:(half + 1) * 512], op0=ALU.mult, op1=ALU.add),
                       reads=[poB, hB[b]], writes=[hB[b]])

        class Ctx:
            pass

        def run_gu(gu, hooks):
            for j in range(FC):
                gu(j, j + 1)
                for f in hooks.get(j, ()):
                    f()

        def common_ctx(stack, n_xnT):
            ctx = Ctx()
            ctx.junk = sb(stack, "junk", [128, D], BF16); ctx.junkB = Buf("junk")
            ctx.ssq = sb(stack, "ssq", [128, 4], F32); ctx.ssqB = [Buf("ssq%d" % b) for b in range(4)]
            ctx.rstd = sb(stack, "rstd", [128, 4], F32); ctx.rstdB = [Buf("rstd%d" % b) for b in range(4)]
            ctx.lnv = sb(stack, "lnv", [128, 4], F32); ctx.lnvB = Buf("lnv")
            ctx.xn = [sb(stack, "xn%d" % i, [128, D], BF16) for i in range(4)]
            ctx.xnB = [Buf("xn%d" % i) for i in range(4)]
            ctx.xnT = [sb(stack, "xnT%d" % k, [128, DC, T], BF16) for k in range(n_xnT)]
            ctx.xnTB = [[Buf("xnT%d_%d" % (k, b)) for b in range(4)] for k in range(n_xnT)]
            ctx.aT = sb(stack, "aT", [128, FC, T], BF16); ctx.aTB = [Buf("aT%d" % j) for j in range(FC)]
            ctx.sg = [sb(stack, "sg%d" % i, [128, T], BF16) for i in range(2)]
            ctx.sgB = [Buf("sg%d" % i) for i in range(2)]
            ctx.wd = sb(stack, "wd", [128, FC * D], BF16); ctx.wdB = [Buf("wd%d" % k) for k in range(4)]
            ctx.hbuf = [sb(stack, "h%d" % i, [128, 4, D], F32) for i in range(3)]
            ctx.hB = [[Buf("h%d_%d" % (i, b)) for b in range(4)] for i in range(3)]
            return ctx

        tiles = []
        for s, Sl in enumerate(seqs):
            for t in range(Sl // T):
                tiles.append((s, t))
        nblk_s = [Sl // 128 for Sl in seqs]
        blk0_s = [toff[s] // 128 for s in range(len(seqs))]

        h1B = {tl: Buf("h1_%d_%d" % tl) for tl in tiles}
        qB = {tl: Buf("q_%d_%d" % tl) for tl in tiles}
        kB = {tl: Buf("k_%d_%d" % tl) for tl in tiles}
        vB = {tl: Buf("v_%d_%d" % tl) for tl in tiles}
        pB = {tl: Buf("p_%d_%d" % tl) for tl in tiles}
        aoB = {tl: Buf("ao_%d_%d" % tl) for tl in tiles}
        metaKB, metaVB = Buf("kTm"), Buf("vm")
        pEdgeB = [Buf("pedge%d" % s) for s in range(len(seqs))]

        with ExitStack() as ps:
            ctx = common_ctx(ps, 2)
            winS = sb(ps, "winS", [128, DC * 2048], BF16); winSB = Buf("winS")
            stq = [sb(ps, "stq%d" % i, [128, T], BF16) for i in range(2)]; stqB = [Buf("stq%d" % i) for i in range(2)]
            stp = [sb(ps, "stp%d" % i, [128, T], F32) for i in range(2)]; stpB = [Buf("stp%d" % i) for i in range(2)]
            vst = sb(ps, "vst", [128, 4, NH, 129], BF16); vstB = Buf("vst")
            op("dve", lambda e: e.memset(vst[:], 1.0), writes=[vstB])
            jobs = [("meta", None)] + [("tile", tl) for tl in tiles]
            ws = WStream(ps, [(wgu_d[0], j, wB["wgu1"]) for _ in jobs for j in range(FC)])
            ws.prime()
            for s, Sl in enumerate(seqs):
                op("pool", lambda e, s=s, Sl=Sl: e.dma_start(
                    out=pT_d.ap()[:, :, pcol[s] + 8 + Sl:pcol[s] + 16 + Sl].rearrange("g p n -> p g n"), in_=zeros8[:]),
                   reads=[zB], writes=[pEdgeB[s]], dma=True)

            J = len(jobs)

            def jinfo(ji):
                kind, tl = jobs[ji]
                nb = 1 if kind == "meta" else 4
                return kind, tl, ctx.hbuf[ji % 3], ctx.hB[ji % 3], nb, nb * 128

            def load_x(ji):
                kind, tl, h, hB, nb, N = jinfo(ji)
                if kind == "meta":
                    op("dve", lambda e: e.memset(h[:, 0, :], 0.0), writes=[hB[0]])
                    op("sp", lambda e: e.dma_start(out=h[0:NMETA, 0, :], in_=meta.ap()), writes=[hB[0]], dma=True)
                else:
                    s, t = tl
                    tok0 = toff[s] + t * T
                    op("sp", lambda e: e.dma_start(out=h[:], in_=x.ap()[tok0:tok0 + T, :].rearrange("(b p) d -> p b d", p=128)),
                       writes=hB, dma=True)

            def N1(ji):
                kind, tl, h, hB, nb, N = jinfo(ji)
                emit_norm_act(ctx, h, hB, nb)
                emit_norm_pe(ctx, nb, ctx.xnT[0], ctx.xnTB[0])

            def GU(ji, j0, j1):
                kind, tl, h, hB, nb, N = jinfo(ji)
                emit_gu(ctx, nb, N, ws, ctx.xnT[0], ctx.xnTB[0], j0, j1)

            def DN(ji):
                kind, tl, h, hB, nb, N = jinfo(ji)
                emit_dn(ctx, h, hB, nb)
                if kind == "tile":
                    s, t = tl
                    tok0 = toff[s] + t * T
                    op("pool", lambda e: e.dma_start(
                        out=h1_d.ap()[tok0:tok0 + T, :].rearrange("(b p) d -> p b d", p=128), in_=h[:]),
                       reads=hB, writes=[h1B[tl]], dma=True)

            def NMact(ji):
                kind, tl, h, hB, nb, N = jinfo(ji)
                emit_norm_act(ctx, h, hB, nb)

            def NMpe(ji):
                kind, tl, h, hB, nb, N = jinfo(ji)
                emit_norm_pe(ctx, nb, ctx.xnT[1], ctx.xnTB[1])

            qi = [0]

            def WI(ji):
                kind, tl, h, hB, nb, N = jinfo(ji)
                xnT = ctx.xnT[1]
                xr = [ctx.xnTB[1][b] for b in range(nb)]
                if kind == "tile":
                    s, t = tl
                    tok0 = toff[s] + t * T
                for j in range(8):
                    if kind == "meta" and j < 4:
                        continue
                    pq, pqB = mmp.get()

                    def mmq(e, pq=pq, j=j):
                        ins = None
                        for c in range(DC):
                            ins = e.matmul(pq[:, 0:N], lhsT=winS[:, c * 2048 + j * 128:c * 2048 + (j + 1) * 128],
                                           rhs=xnT[:, c, 0:N], start=(c == 0), stop=(c == DC - 1))
                        return ins
                    op("pe", mmq, reads=[winSB] + xr, writes=[pqB])
                    k2 = qi[0] % 2
                    qi[0] += 1
                    if k2:
                        op("act", lambda e, pq=pq, k2=k2: e.activation(out=stq[k2][:, 0:N], in_=pq[:, 0:N], func=AF.Copy),
                           reads=[pqB], writes=[stqB[k2]])
                    else:
                        op("dve", lambda e, pq=pq, k2=k2: e.tensor_copy(out=stq[k2][:, 0:N], in_=pq[:, 0:N]),
                           reads=[pqB], writes=[stqB[k2]])
                    if kind == "meta":
                        op("pool", lambda e, k2=k2, j=j: e.dma_start(out=kTm_d.ap()[j - 4], in_=stq[k2][:, 0:NMETA]),
                           reads=[stqB[k2]], writes=[metaKB], dma=True)
                    else:
                        dst = qT_d if j < 4 else kT_d
                        dB = qB[tl] if j < 4 else kB[tl]
                        op("pool", lambda e, k2=k2, j=j, dst=dst: e.dma_start(
                            out=dst.ap()[j % 4][:, tok0:tok0 + T], in_=stq[k2][:, :]),
                           reads=[stqB[k2]], writes=[dB], dma=True)
                for g in range(4):
                    pq, pqB = mmp.get()

                    def mmp_(e, pq=pq, g=g):
                        ins = None
                        for c in range(DC):
                            o = c * 2048 + 1536 + g * 128
                            ins = e.matmul(pq[:, 0:N], lhsT=winS[:, o:o + 128], rhs=xnT[:, c, 0:N],
                                           start=(c == 0), stop=(c == DC - 1))
                        return ins
                    op("pe", mmp_, reads=[winSB] + xr, writes=[pqB])
                    k2 = g % 2
                    op("act", lambda e, pq=pq, k2=k2: e.activation(out=stp[k2][:, 0:N], in_=pq[:, 0:N], func=AF.Copy),
                       reads=[pqB], writes=[stpB[k2]])
                    if kind == "meta":
                        for s2 in range(len(seqs)):
                            op("pool", lambda e, k2=k2, g=g, s2=s2: e.dma_start(
                                out=pT_d.ap()[g][:, pcol[s2]:pcol[s2] + 8], in_=stp[k2][:, 8:16]),
                               reads=[stpB[k2]], writes=[pEdgeB[s2]], dma=True)
                    else:
                        c0 = pcol[s] + 8 + t * T
                        op("pool", lambda e, k2=k2, g=g, c0=c0: e.dma_start(out=pT_d.ap()[g][:, c0:c0 + T], in_=stp[k2][:, :]),
                           reads=[stpB[k2]], writes=[pB[tl]], dma=True)
                for b in range(nb):
                    pv, pvB = mmp.get()

                    def mmv(e, pv=pv, b=b):
                        ins = None
                        for c in range(DC):
                            ins = e.matmul(pv[:, :], lhsT=xnT[:, c, b * 128:(b + 1) * 128],
                                           rhs=winS[:, c * 2048 + 1024:c * 2048 + 1536], start=(c == 0), stop=(c == DC - 1))
                        return ins
                    op("pe", mmv, reads=[winSB, ctx.xnTB[1][b]], writes=[pvB])
                    op("dve", lambda e, pv=pv, b=b: e.tensor_copy(out=vst[:, b, :, 0:128],
                                                                 in_=pv[:, :].rearrange("p (h d) -> p h d", h=NH)),
                       reads=[pvB], writes=[vstB])
                if kind == "meta":
                    op("pool", lambda e: e.dma_start(out=vm_d.ap(), in_=vst[0:NMETA, 0, :, :].rearrange("p h d -> p (h d)")),
                       reads=[vstB], writes=[metaVB], dma=True)
                else:
                    blk0 = tok0 // 128
                    for hh in range(NH):
                        op("pool", lambda e, hh=hh, blk0=blk0: e.dma_start(
                            out=v_d.ap()[hh][:, blk0 * 129:(blk0 + 4) * 129].rearrange("p (b d) -> p b d", b=4),
                            in_=vst[:, :, hh, :]),
                           reads=[vstB], writes=[vB[tl]], dma=True)

            load_x(0)
            if J > 1:
                load_x(1)
            load_wd(ctx, 0)
            op("sp", lambda e: e.dma_start(out=winS[:, 0:8192], in_=win_d.ap()[:, 0:8192]), reads=[wB["win"]], writes=[winSB], dma=True)
            op("sp", lambda e: e.dma_start(out=winS[:, 8192:16384], in_=win_d.ap()[:, 8192:16384]), reads=[wB["win"]], writes=[winSB], dma=True)
            N1(0)
            if J > 2:
                load_x(2)
            GU(0, 0, FC)
            if J > 1:
                N1(1)
            DN(0)
            def NMact1(ji):
                kind, tl, h, hB, nb, N = jinfo(ji)
                emit_norm_act1(ctx, h, hB, nb)

            def NMact2(ji):
                kind, tl, h, hB, nb, N = jinfo(ji)
                emit_norm_act2(ctx, h, hB, nb)

            def N1act1(ji):
                kind, tl, h, hB, nb, N = jinfo(ji)
                emit_norm_act1(ctx, h, hB, nb)

            def N1act2(ji):
                kind, tl, h, hB, nb, N = jinfo(ji)
                emit_norm_act2(ctx, h, hB, nb)

            def N1pe(ji):
                kind, tl, h, hB, nb, N = jinfo(ji)
                emit_norm_pe(ctx, nb, ctx.xnT[0], ctx.xnTB[0])

            for ji in range(J):
                nxt = ji + 1 < J
                hooks = {0: [lambda: NMact1(ji)], 2: [lambda: NMact2(ji)], 8: [lambda: NMpe(ji)]}
                if ji + 3 < J:
                    hooks[7] = [lambda: load_x(ji + 3)]
                if ji + 2 < J:
                    hooks[10] = [lambda: N1act1(ji + 2)]
                    hooks[12] = [lambda: N1act2(ji + 2)]
                if nxt:
                    run_gu(lambda j0, j1: GU(ji + 1, j0, j1), hooks)
                else:
                    for j in sorted(hooks):
                        for f in hooks[j]:
                            f()
                WI(ji)
                if ji + 2 < J:
                    N1pe(ji + 2)
                if nxt:
                    DN(ji + 1)
        S.barrier()

        with ExitStack() as ps:
            KW = NMETA + SMAX
            NBM = SMAX // 128
            kT = [sb(ps, "kT%d" % i, [128, KW], BF16) for i in range(2)]
            qT = [sb(ps, "qT%d" % i, [128, SMAX], BF16) for i in range(2)]
            Vs = [sb(ps, "V%d" % i, [128, NBM * 129], BF16) for i in range(2)]
            bia = [sb(ps, "bia%d" % i, [128, 6 * 512], F32) for i in range(2)]
            biam = [sb(ps, "biam%d" % i, [NMETA, 512], F32) for i in range(2)]
            kTB = [Buf("kT%d" % i) for i in range(2)]
            qTB = [Buf("qT%d" % i) for i in range(2)]
            VB = [Buf("V%d" % i) for i in range(2)]
            biaB = [Buf("bia%d" % i) for i in range(2)]
            vmt = sb(ps, "vmt", [NMETA, NH * 129], BF16); vmtB = Buf("vmt")
            op("sp", lambda e: e.dma_start(out=vmt[:], in_=vm_d.ap()), reads=[metaVB], writes=[vmtB], dma=True)
            PT = [sb(ps, "PT%d" % i, [128, 2, 512], BF16) for i in range(3)]
            PTB = [Buf("PT%d" % i) for i in range(3)]
            tmp2 = [sb(ps, "tmp%d" % i, [128, 2, 512], F32) for i in range(2)]
            tmp2B = [[Buf("tmp%d_%d" % (i, c)) for c in range(2)] for i in range(2)]
            accSf = sb(ps, "accS", [128, 8 * 130], F32)
            accS = accSf[:].rearrange("p (k n) -> p k n", n=130)
            accSBk = [Buf("accS%d" % i) for i in range(3)]
            rr8 = sb(ps, "rr8", [128, 8], F32); rr8B = Buf("rr8")
            ssq4 = sb(ps, "ssq4", [128, 4], F32); ssq4B = Buf("ssq4")
            ln4 = sb(ps, "ln4", [128, 4], F32); ln4B = Buf("ln4")
            rs = sb(ps, "rs", [128, 4], F32); rsB = Buf("rs")
            ob = sb(ps, "ob", [128, 4, 128], F32); obB = Buf("ob")
            ob2 = sb(ps, "ob2", [128, 4, 128], F32); ob2B = Buf("ob2")
            on = sb(ps, "on", [128, 4, 128], BF16); onB = Buf("on")
            pend_a, pend_b = [], []
            bg32 = sb(ps, "bg32", [128, 2 * DC * 512], F32); bg32B = Buf("bg32")
            bg16 = sb(ps, "bg16", [128, 4 * 2048], BF16); bg16B = Buf("bg16")
            bs32, bs32B, bs16, bs16B = bg32, bg32B, bg16, bg16B
            bg_tasks = []

            def add_bg(tasks, gap):
                bg_tasks.append(tasks[0])
                bg_tasks.extend([None] * gap)
                bg_tasks.extend(tasks[1:])
            for j0 in range(0, FC, 4):
                add_bg(gu_tasks(1, j0, bg32, bg32B, bg16, bg16B, True), 18)
            for j2 in range(FC // 2):
                add_bg(wd_tasks(1, j2, bs32, bs32B, bs16, bs16B, True), 6)
            for c2 in range(DC // 2):
                add_bg(wout_tasks(c2, bs32, bs32B, bs16, bs16B, True), 6)
            add_bg(poolw_tasks(bs32, bs32B, bs16, bs16B, True), 6)
            bg_tasks.reverse()
            bgc = [0]

            def bg_tick():
                bgc[0] += 1
                if bg_tasks and bgc[0] % 2 == 0:
                    f = bg_tasks.pop()
                    if f is not None:
                        f()
            aoS = [sb(ps, "aoS%d" % i, [128, 512], BF16) for i in range(2)]
            aoSB = [Buf("aoS%d" % i) for i in range(2)]
            lt = sb(ps, "lt", [128, 4], F32); ltB = Buf("lt")
            ljunk = sb(ps, "ljunk", [128, 64], F32)
            for k in range(2):
                op("dve", lambda e, k=k: e.scalar_tensor_tensor(out=ljunk[:], in0=lam4[:, 2 * k, :], scalar=1.0,
                                                                in1=lam4[:, 2 * k + 1, :], op0=ALU.mult, op1=ALU.mult,
                                                                accum_out=lt[:, k:k + 1]),
                   reads=[lamB], writes=[ltB])
            op("act", lambda e: e.activation(out=lt[:, 2:4], in_=lt[:, 0:2], func=AF.Exp), reads=[ltB], writes=[ltB])
            op("dve", lambda e: e.tensor_tensor(out=neglam[:], in0=lt[:, 3:4], in1=lt[:, 2:3], op=ALU.subtract),
               reads=[ltB], writes=[neglamB])
            op("dve", lambda e: e.tensor_scalar(out=neglam[:], in0=neglam[:], scalar1=-LAMBDA_INIT, scalar2=None, op0=ALU.add),
               reads=[neglamB], writes=[neglamB])

            ACC = [4, 5, 6]
            TRB = 7
            trbank = banks[TRB][:].bitcast(BF16)

            def acc_ap(k):
                return banks[ACC[k // 3]][:, (k % 3) * 130:(k % 3) * 130 + 129]

            pairs = [(s, hh) for s in range(len(seqs)) for hh in range(NH)]

            def load_pair(pi):
                s, hh = pairs[pi]
                par = pi % 2
                Sl = seqs[s]
                op("sp", lambda e: e.dma_start(out=kT[par][:, 0:NMETA], in_=kTm_d.ap()[hh]),
                   reads=[metaKB], writes=[kTB[par]], dma=True)
                tls = [(s, t) for t in range(Sl // T)]
                op("sp", lambda e: e.dma_start(out=kT[par][:, NMETA:NMETA + Sl], in_=kT_d.ap()[hh][:, toff[s]:toff[s] + Sl]),
                   reads=[kB[tl] for tl in tls] + [kTB[par]], writes=[kTB[par]], dma=True)
                op("sp", lambda e: e.dma_start(out=qT[par][:, 0:Sl], in_=qT_d.ap()[hh][:, toff[s]:toff[s] + Sl]),
                   reads=[qB[tl] for tl in tls], writes=[qTB[par]], dma=True)
                nbs = nblk_s[s]
                op("sp", lambda e: e.dma_start(out=Vs[par][:, 0:nbs * 129],
                                               in_=v_d.ap()[hh][:, blk0_s[s] * 129:(blk0_s[s] + nbs) * 129]),
                   reads=[vB[tl] for tl in tls], writes=[VB[par]], dma=True)
                op("sp", lambda e: e.dma_start(out=bia[par][:], in_=bias_d.ap()[hh]), writes=[biaB[par]], dma=True)
                op("sp", lambda e: e.dma_start(out=biam[par][:], in_=biasm_d.ap()[hh]), reads=[biaB[par]], writes=[biaB[par]], dma=True)

            steps = []
            for pi, (s_, hh_) in enumerate(pairs):
                kbs_ = [-1] + list(range(nblk_s[s_]))
                for qt_ in range(seqs[s_] // T):
                    for idx_, kb_ in enumerate(kbs_):
                        steps.append((pi, s_, hh_, qt_, kb_, idx_, len(kbs_)))

            def emit_qk(g):
                pi, s, hh, qt, kb, idx, nk = steps[g]
                par = pi % 2
                q0 = qt * T
                p2 = g % 2
                np_ = NMETA if kb < 0 else 128
                k0 = 0 if kb < 0 else NMETA + kb * 128

                def mm(e):
                    ins = None
                    for c in range(2):
                        ins = e.matmul(banks[2 * p2 + c][0:np_, :], lhsT=kT[par][c * 64:(c + 1) * 64, k0:k0 + np_],
                                       rhs=qT[par][c * 64:(c + 1) * 64, q0:q0 + T], start=True, stop=True)
                    return ins
                op("pe", mm, reads=[kTB[par], qTB[par]], writes=[bankB[2 * p2], bankB[2 * p2 + 1]])

            def emit_exp(g):
                pi, s, hh, qt, kb, idx, nk = steps[g]
                par = pi % 2
                p2 = g % 2
                p3 = g % 3
                np_ = NMETA if kb < 0 else 128
                delta = kb - 4 * qt
                if kb < 0:
                    near = (qt == 0)
                    btile = biam[par][0:NMETA, :]
                    side = 0
                else:
                    near = (-1 <= delta <= 4)
                    btile = bia[par][:, (delta + 1) * 512:(delta + 2) * 512] if near else None
                    side = 0 if delta < 0 else 1
                sbk = [bankB[2 * p2], bankB[2 * p2 + 1]]
                tmp, tmpB = tmp2[p2], tmp2B[p2]
                if near:
                    op("dve", lambda e: e.tensor_tensor(
                        out=tmp[0:np_, :, :], in0=pall[0:np_, 2 * p2:2 * p2 + 2, :],
                        in1=btile.unsqueeze(1).broadcast_to([np_, 2, 512]), op=ALU.add),
                       reads=sbk + [biaB[par]], writes=tmpB)
                    op("act", lambda e: e.activation(out=PT[p3][0:np_, :, :], in_=tmp[0:np_, :, :], func=AF.Exp,
                                                     scale=0.125),
                       reads=tmpB, writes=[PTB[p3]])
                else:
                    op("act", lambda e: e.activation(
                        out=PT[p3][0:np_, :, :], in_=pall[0:np_, 2 * p2:2 * p2 + 2, :], func=AF.Exp,
                        bias=cb[0:np_, side, hh:hh + 1], scale=0.125),
                       reads=sbk + [cbB], writes=[PTB[p3]])

            def emit_av(g):
                pi, s, hh, qt, kb, idx, nk = steps[g]
                par = pi % 2
                p3 = g % 3
                first = (idx == 0)
                last = (idx == nk - 1)

                def mm(e):
                    ins = None
                    for c in range(2):
                        for i in range(4):
                            k = c * 4 + i
                            st_flag = first and (k % 3 == 0)
                            if kb < 0:
                                ins = e.matmul(acc_ap(k), lhsT=PT[p3][0:NMETA, c, i * 128:(i + 1) * 128],
                                               rhs=vmt[0:NMETA, hh * 129:(hh + 1) * 129],
                                               start=st_flag, stop=last, skip_group_check=True)
                            else:
                                ins = e.matmul(acc_ap(k), lhsT=PT[p3][:, c, i * 128:(i + 1) * 128],
                                               rhs=Vs[par][:, kb * 129:(kb + 1) * 129],
                                               start=st_flag, stop=last, skip_group_check=True)
                    return ins
                op("pe", mm, reads=[PTB[p3], VB[par], vmtB], writes=[bankB[b_] for b_ in ACC])

            load_pair(0)
            qtc = [0]
            emit_qk(0)
            emit_qk(1)
            for g in range(len(steps)):
                pi, s, hh, qt, kb, idx, nk = steps[g]
                q0 = qt * T
                if idx == 0 and qt == 0 and pi + 1 < len(pairs):
                    load_pair(pi + 1)
                emit_exp(g)
                if g + 2 < len(steps):
                    emit_qk(g + 2)
                emit_av(g)
                bg_tick()
                if idx == 5 and pend_a:
                    pend_a.pop()()
                if idx == 9 and pend_b:
                    pend_b.pop()()
                if idx == nk - 1:
                    if pend_a:
                        pend_a.pop()()
                    if pend_b:
                        pend_b.pop()()
                    for b_ in range(3):
                        ncol = 390 if b_ < 2 else 260
                        op("dve", lambda e, b_=b_, ncol=ncol: e.tensor_copy(
                            out=accSf[:, b_ * 390:b_ * 390 + ncol], in_=banks[ACC[b_]][:, 0:ncol]),
                           reads=[bankB[ACC[b_]]], writes=[accSBk[b_]])
                    a2 = qtc[0] % 2
                    qtc[0] += 1
                    tok0 = toff[s] + q0

                    def ep_a(hh=hh):
                        op("dve", lambda e: e.reciprocal(out=rr8[:, :], in_=accS[:, :, 128]), reads=accSBk, writes=[rr8B])
                        op("dve", lambda e: e.tensor_scalar(out=rr8[:, 4:8], in0=rr8[:, 4:8], scalar1=neglam[:, 0:1],
                                                            scalar2=None, op0=ALU.mult),
                           reads=[rr8B, neglamB], writes=[rr8B])
                        op("dve", lambda e: e.tensor_tensor(out=ob[:, :, :], in0=accS[:, 0:4, 0:128],
                                                            in1=rr8[:, 0:4].unsqueeze(2).broadcast_to([128, 4, 128]), op=ALU.mult),
                           reads=accSBk + [rr8B], writes=[obB])
                        op("dve", lambda e: e.tensor_tensor(out=ob2[:, :, :], in0=accS[:, 4:8, 0:128],
                                                            in1=rr8[:, 4:8].unsqueeze(2).broadcast_to([128, 4, 128]), op=ALU.mult),
                           reads=accSBk + [rr8B], writes=[ob2B])
                        op("dve", lambda e: e.tensor_tensor(out=ob[:, :, :], in0=ob[:, :, :], in1=ob2[:, :, :], op=ALU.add),
                           reads=[obB, ob2B], writes=[obB])
                        op("dve", lambda e: e.tensor_tensor(out=ob2[:, :, :], in0=ob[:, :, :], in1=ob[:, :, :], op=ALU.mult),
                           reads=[obB], writes=[ob2B])
                        op("dve", lambda e: e.reduce_sum(out=ssq4[:, :], in_=ob2[:, :, :], axis=mybir.AxisListType.X),
                           reads=[ob2B], writes=[ssq4B])
                        op("act", lambda e: e.activation(out=ln4[:, :], in_=ssq4[:, :], func=AF.Ln, bias=eps_t[:, 0:1], scale=1.0 / 128),
                           reads=[ssq4B, zB], writes=[ln4B])
                        op("act", lambda e: e.activation(out=rs[:, :], in_=ln4[:, :], func=AF.Exp, scale=-0.5),
                           reads=[ln4B], writes=[rsB])
                        op("dve", lambda e: e.tensor_tensor(out=on[:, :, :], in0=ob[:, :, :],
                                                            in1=rs[:, 0:4].unsqueeze(2).broadcast_to([128, 4, 128]), op=ALU.mult),
                           reads=[obB, rsB], writes=[onB])

                    def ep_b(hh=hh, a2=a2, tok0=tok0, s=s, qt=qt):
                        def trs(e):
                            ins = None
                            for i in range(4):
                                ins = e.transpose(trbank[:, i * 128:(i + 1) * 128], on[:, i, :], ident[:])
                            return ins
                        op("pe", trs, reads=[onB, identB], writes=[bankB[TRB]])
                        op("dve", lambda e: e.tensor_copy(out=aoS[a2][:], in_=trbank[:, 0:512]),
                           reads=[bankB[TRB]], writes=[aoSB[a2]])
                        op("pool", lambda e: e.dma_start(out=ao_d.ap()[hh][:, tok0:tok0 + T], in_=aoS[a2][:]),
                           reads=[aoSB[a2]], writes=[aoB[(s, qt)]], dma=True)
                    if pend_a:
                        pend_a.pop()()
                    if pend_b:
                        pend_b.pop()()
                    pend_a.append(ep_a)
                    pend_b.append(ep_b)
            if pend_a:
                pend_a.pop()()
            if pend_b:
                pend_b.pop()()
            while bg_tasks:
                f = bg_tasks.pop()
                if f is not None:
                    f()
        S.barrier()

        with ExitStack() as ps:
            ctx = common_ctx(ps, 1)
            woutS = sb(ps, "woutS", [128, DC * D], BF16); woutSB = Buf("woutS")
            op("sp", lambda e: e.dma_start(out=woutS[:], in_=wout_d.ap()), reads=[wB["wout"]], writes=[woutSB], dma=True)
            pwS = sb(ps, "pwS", [128, 512], BF16); pwSB = Buf("pwS")
            op("sp", lambda e: e.dma_start(out=pwS[:], in_=poolw_d.ap()), reads=[wB["poolw"]], writes=[pwSB], dma=True)
            mixT = sb(ps, "mixT", [128, DC, T], BF16)
            mixAB = Buf("mixA")
            mixPB = [Buf("mixP%d" % g) for g in range(4)]
            ph = sb(ps, "ph", [128, 4, T + 16], F32)
            phB = Buf("ph")
            pt1 = sb(ps, "pt1", [128, T + 16], F32); pt2 = sb(ps, "pt2", [128, T + 16], F32)
            pt1B, pt2B = Buf("pt1"), Buf("pt2")
            pooled = sb(ps, "pooled", [128, 4, T], BF16); pooledB = [Buf("pooled%d" % g) for g in range(4)]
            ws = WStream(ps, [(wgu_d[1], j, wB["wgu2"]) for _ in tiles for j in range(FC)], ns=6)
            ws.prime()
            n_t = len(tiles)

            def tinfo(ti):
                s, t = tiles[ti]
                return s, t, toff[s] + t * T, ctx.hbuf[ti % 3], ctx.hB[ti % 3]

            def LDh(ti):
                s, t, tok0, h, hB = tinfo(ti)
                op("sp", lambda e: e.dma_start(out=h[:], in_=h1_d.ap()[tok0:tok0 + T, :].rearrange("(b p) d -> p b d", p=128)),
                   reads=[h1B[(s, t)]], writes=hB, dma=True)

            def LDm(ti):
                s, t, tok0, h, hB = tinfo(ti)
                op("sp", lambda e: e.dma_start(out=mixT[:, 0:4, :], in_=ao_d.ap()[:, :, tok0:tok0 + T].rearrange("h p n -> p h n")),
                   reads=[aoB[(s, t)]], writes=[mixAB], dma=True)
                c0 = pcol[s] + t * T
                rds = [pB[(s, t)], pEdgeB[s]]
                if t > 0:
                    rds.append(pB[(s, t - 1)])
                if (s, t + 1) in pB:
                    rds.append(pB[(s, t + 1)])
                op("sp", lambda e: e.dma_start(out=ph[:], in_=pT_d.ap()[:, :, c0:c0 + T + 16].rearrange("g p n -> p g n")),
                   reads=rds, writes=[phB], dma=True)

            def PLdve(ti, groups=(0, 1, 2, 3)):
                s, t, tok0, h, hB = tinfo(ti)
                P = ph
                last_tile = (t == seqs[s] // T - 1)
                for g, w in enumerate((2, 4, 8, 16)):
                    if g not in groups:
                        continue
                    a = 8 - w // 2
                    lens = T + w - 2
                    cur, curB = pt1, pt1B
                    nxt, nxtB = pt2, pt2B
                    op("dve", lambda e, g=g, a=a, lens=lens, cur=cur: e.tensor_tensor(
                        out=cur[:, 0:lens], in0=P[:, g, a:a + lens], in1=P[:, g, a + 1:a + 1 + lens], op=ALU.add),
                       reads=[phB], writes=[curB])
                    sh = 2
                    while sh < w:
                        lens -= sh
                        op("dve", lambda e, lens=lens, sh=sh, cur=cur, nxt=nxt: e.tensor_tensor(
                            out=nxt[:, 0:lens], in0=cur[:, 0:lens], in1=cur[:, sh:sh + lens], op=ALU.add),
                           reads=[curB], writes=[nxtB])
                        cur, curB, nxt, nxtB = nxt, nxtB, cur, curB
                        sh *= 2
                    op("dve", lambda e, g=g, w=w, cur=cur: e.scalar_tensor_tensor(
                        out=pooled[:, g, :], in0=cur[:, 0:T], scalar=1.0 / w, in1=P[:, g, 8:8 + T],
                        op0=ALU.mult, op1=ALU.subtract),
                       reads=[curB, phB], writes=[pooledB[g]])
                    if last_tile:
                        for tt in range(T - w // 2 + 1, T):
                            cntv = w - (tt + w // 2 - T)
                            op("dve", lambda e, g=g, tt=tt, cntv=cntv, cur=cur: e.scalar_tensor_tensor(
                                out=pooled[:, g, tt:tt + 1], in0=cur[:, tt:tt + 1], scalar=1.0 / cntv,
                                in1=P[:, g, 8 + tt:9 + tt], op0=ALU.mult, op1=ALU.subtract),
                               reads=[curB, phB], writes=[pooledB[g]])

            def WO(ti):
                s, t, tok0, h, hB = tinfo(ti)
                for g in range(4):
                    pq, pqB = mmp.get()
                    op("pe", lambda e, pq=pq, g=g: e.matmul(pq[:, :], lhsT=pwS[:, g * 128:(g + 1) * 128], rhs=pooled[:, g, :],
                                                           start=True, stop=True),
                       reads=[pwSB, pooledB[g]], writes=[pqB])
                    op("act", lambda e, pq=pq, g=g: e.activation(out=mixT[:, 4 + g, :], in_=pq[:, :], func=AF.Copy),
                       reads=[pqB], writes=[mixPB[g]])
                for b in range(4):
                    for half in range(2):
                        po, poB = mmp.get()

                        def mmo(e, po=po, b=b, half=half):
                            ins = None
                            for c in range(DC):
                                ins = e.matmul(po[:, :], lhsT=mixT[:, c, b * 128:(b + 1) * 128],
                                               rhs=woutS[:, c * D + half * 512:c * D + (half + 1) * 512],
                                               start=(c == 0), stop=(c == DC - 1))
                            return ins
                        op("pe", mmo, reads=[woutSB, mixAB] + mixPB, writes=[poB])
                        op("dve", lambda e, po=po, b=b, half=half: e.tensor_tensor(
                            out=h[:, b, half * 512:(half + 1) * 512], in0=po[:, :], in1=h[:, b, half * 512:(half + 1) * 512],
                            op=ALU.add),
                           reads=[poB, hB[b]], writes=[hB[b]])

            def N2act(ti):
                s, t, tok0, h, hB = tinfo(ti)
                emit_norm_act(ctx, h, hB, 4)

            def N2pe(ti):
                emit_norm_pe(ctx, 4, ctx.xnT[0], ctx.xnTB[0])

            def GU(ti, j0, j1):
                emit_gu(ctx, 4, T, ws, ctx.xnT[0], ctx.xnTB[0], j0, j1)

            def DN(ti):
                s, t, tok0, h, hB = tinfo(ti)
                emit_dn(ctx, h, hB, 4)

            def FINact(ti):
                s, t, tok0, h, hB = tinfo(ti)
                for b in range(4):
                    op("act", lambda e, b=b: e.activation(out=ctx.junk[:], in_=h[:, b, :], func=AF.Square,
                                                          accum_out=ctx.ssq[:, b:b + 1]),
                       reads=[hB[b]], writes=[ctx.junkB, ctx.ssqB[b]])
                emit_rstd(ctx, 4, D)

            def FINb(ti, b):
                s, t, tok0, h, hB = tinfo(ti)
                op("dve", lambda e: e.scalar_tensor_tensor(out=h[:, b, :], in0=h[:, b, :], scalar=ctx.rstd[:, b:b + 1],
                                                           in1=gfin_bc[:], op0=ALU.mult, op1=ALU.mult),
                   reads=[hB[b], ctx.rstdB[b], gfinB], writes=[hB[b]])

            def FINst(ti):
                s, t, tok0, h, hB = tinfo(ti)
                op("pool", lambda e: e.dma_start(
                    out=y.ap()[tok0:tok0 + T, :].rearrange("(b p) d -> p b d", p=128), in_=h[:]),
                   reads=hB, writes=[Buf("ydump")], dma=True)

            LDh(0); LDm(0)
            load_wd(ctx, 1)
            if n_t > 1:
                LDh(1)
            PLdve(0); WO(0)
            if n_t > 1:
                LDm(1)
            N2act(0); N2pe(0)
            if n_t > 2:
                LDh(2)
            def FIN(ti):
                FINact(ti)
                for b in range(4):
                    FINb(ti, b)
                FINst(ti)

            for ti in range(n_t):
                nxt = ti + 1 < n_t
                hooks = {}
                if ti > 0:
                    hooks[1] = [lambda: FINact(ti - 1)]
                    for b in range(4):
                        hooks.setdefault(3 + b, []).append(lambda b=b: FINb(ti - 1, b))
                    hooks[6].append(lambda: FINst(ti - 1))
                    if ti + 2 < n_t:
                        hooks.setdefault(15, []).append(lambda: LDh(ti + 2))
                if nxt:
                    for g in range(4):
                        hooks.setdefault(8 + 2 * g, []).append(lambda g=g: PLdve(ti + 1, (g,)))
                run_gu(lambda j0, j1: GU(ti, j0, j1), hooks)
                if nxt:
                    WO(ti + 1)
                    if ti + 2 < n_t:
                        LDm(ti + 2)
                    N2act(ti + 1)
                DN(ti)
                if nxt:
                    N2pe(ti + 1)
            FIN(n_t - 1)
        S.final_wait("sp")
        S.final_wait("pool")
    return nc


_CACHE = {}


def _core_inputs(inputs, xs):
    c = host_consts()
    f = lambda a: np.ascontiguousarray(np.asarray(a, dtype=np.float32))
    m = {
        "x": np.ascontiguousarray(xs), "meta": f(inputs["meta_tokens"]), "table": f(inputs["rel_bias_table"]),
        "g1": f(inputs["norm_ffn1"][0]), "gmix": f(inputs["norm_mix"][0]), "g2": f(inputs["norm_ffn2"][0]),
        "gfin": f(inputs["norm_final"]),
        "wg1": f(inputs["ffn1_w_gate"][0]), "wu1": f(inputs["ffn1_w_up"][0]), "wd1": f(inputs["ffn1_w_down"][0]),
        "wg2": f(inputs["ffn2_w_gate"][0]), "wu2": f(inputs["ffn2_w_up"][0]), "wd2": f(inputs["ffn2_w_down"][0]),
        "win": f(inputs["w_in"][0]), "wout": f(inputs["w_out"][0]),
        "lq1": f(inputs["lambda_q1"][0]), "lk1": f(inputs["lambda_k1"][0]),
        "lq2": f(inputs["lambda_q2"][0]), "lk2": f(inputs["lambda_k2"][0]),
        "subln": f(inputs["subln_gain"][0]), "poolw": f(inputs["pool_w"][0]), "pscale": f(inputs["pool_scale"][0]),
        "erevT": c["erevT"], "ident": c["ident"], "jflip": c["jflip"],
    }
    return m


def kernel(**inputs):
    xp = np.asarray(inputs["x_prompt"], dtype=np.float32)
    xsm = np.asarray(inputs["x_sample"], dtype=np.float32)
    n = 8
    Bp, Sp, _ = xp.shape
    Bs, Ss, _ = xsm.shape
    ppc = Bp // n
    spc = Bs // n
    seqs = [Sp] * ppc + [Ss] * spc
    key = tuple(seqs)
    if key not in _CACHE:
        _CACHE[key] = build(seqs)
    nc = _CACHE[key]
    in_maps = []
    for c in range(n):
        parts = [xp[c * ppc + i] for i in range(ppc)] + [xsm[c * spc + i] for i in range(spc)]
        in_maps.append(_core_inputs(inputs, np.concatenate(parts, axis=0)))
    res = run_bass_kernel_spmd(nc, in_maps, core_ids=list(range(n)))
    yp = np.empty_like(xp)
    ys = np.empty_like(xsm)
    for c in range(n):
        yc = res.results[c]["y"]
        o = 0
        for i in range(ppc):
            yp[c * ppc + i] = yc[o:o + Sp]
            o += Sp
        for i in range(spc):
            ys[c * spc + i] = yc[o:o + Ss]
            o += Ss
    return (yp, ys)
```
